# Optimizing a Trainium2 kernel written in Bass

```python
import jax, jax.numpy as jnp
from jax import lax
import numpy as np

D_MODEL = 1024
BATCH = 16
SEQ = 2048
DEPTH = 4

CTX_LEN = 256
GRID_W = 64

ML_W = D_MODEL // 2
ML_HEADS = 4
ML_DH = ML_W // ML_HEADS
ML_CHUNK = 128
CV_W = D_MODEL // 4
CV_GROUPS = 4
CV_WIDTH = 3
GM_W = D_MODEL // 4
GM_GROUPS = 4
GM_CHUNK = 128
N_GATES = 4 * ML_HEADS
D_FF = ((8 * D_MODEL // 3 + 255) // 256) * 256
EPS = 1e-6

OFF_K = 0
OFF_V = OFF_K + ML_W
OFF_G = OFF_V + ML_W
OFF_Q = OFF_G + N_GATES
OFF_O = OFF_Q + ML_W
OFF_CV = OFF_O + ML_W
OFF_GM = OFF_CV + 3 * CV_W
D_IN = OFF_GM + 2 * GM_W

kernel_name = "hybrid_mlstm_conv_gmlp_dit"


def _rmsnorm(x, g):
    xf = x.astype(jnp.float32)
    y = xf * lax.rsqrt(jnp.mean(xf * xf, axis=-1, keepdims=True) + EPS) * g.astype(jnp.float32)
    return y.astype(x.dtype)


def _modulation(c_vec, w_ada, b_ada):
    return jnp.split(jax.nn.silu(c_vec) @ w_ada + b_ada, 6, axis=-1)


def _modulate(h, shift, scale):
    return h * (1 + scale) + shift


def _swiglu(h, w1, w3, w2):
    return (jax.nn.silu(h @ w1) * (h @ w3)) @ w2


def _flip(a):
    return jnp.flip(a, axis=2)


def _heads(a):
    bsz, t, _ = a.shape
    return a.reshape(bsz, t, ML_HEADS, ML_DH).transpose(0, 2, 1, 3).astype(jnp.float32)


def _mlstm_kvg(p, b_gates):
    k = _heads(p[..., OFF_K:OFF_K + ML_W]) * (ML_DH ** -0.5)
    v = _heads(p[..., OFF_V:OFF_V + ML_W])
    bsz, t, _ = p.shape
    g = p[..., OFF_G:OFF_G + N_GATES].astype(jnp.float32) + b_gates.astype(jnp.float32)
    g = g.reshape(bsz, t, 4, ML_HEADS).transpose(2, 0, 3, 1)
    li_f, lf_f = g[0], jax.nn.log_sigmoid(g[1])
    li_b, lf_b = g[2], jax.nn.log_sigmoid(g[3])
    return k, v, (li_f, lf_f, li_b, lf_b)


def _mlstm_qo(p):
    q = _heads(p[..., OFF_Q:OFF_Q + ML_W])
    o = jax.nn.sigmoid(p[..., OFF_O:OFF_O + ML_W].astype(jnp.float32))
    return q, o


def _zero_state(bsz):
    return (jnp.zeros((bsz, ML_HEADS, ML_DH, ML_DH), jnp.float32),
            jnp.zeros((bsz, ML_HEADS, ML_DH), jnp.float32),
            jnp.zeros((bsz, ML_HEADS), jnp.float32))


def _mlstm_chunks(q, k, v, li, lf, state):
    bsz, nh, t, dh = q.shape
    nc = t // ML_CHUNK

    def chunk(a):
        a = a.reshape(bsz, nh, nc, ML_CHUNK, *a.shape[3:])
        return jnp.moveaxis(a, 2, 0)

    lower = jnp.tril(jnp.ones((ML_CHUNK, ML_CHUNK), bool))

    def step(carry, inp):
        c_st, n_st, m_st = carry
        qc, kc, vc, lic, lfc = inp
        b = jnp.cumsum(lfc, axis=-1)
        dmat = b[..., :, None] - b[..., None, :] + lic[..., None, :]
        dmat = jnp.where(lower, dmat, -jnp.inf)
        m_inter = b + m_st[..., None]
        m_t = jnp.maximum(m_inter, jnp.max(dmat, axis=-1))
        s = jnp.einsum('bhtd,bhsd->bhts', qc, kc) * jnp.exp(dmat - m_t[..., None])
        inter = jnp.exp(m_inter - m_t)
        num = (jnp.einsum('bhts,bhsd->bhtd', s, vc)
               + inter[..., None] * jnp.einsum('bhtk,bhkv->bhtv', qc, c_st))
        den = jnp.sum(s, axis=-1) + inter * jnp.einsum('bhtk,bhk->bht', qc, n_st)
        h = num / jnp.maximum(jnp.abs(den), jnp.exp(-m_t))[..., None]
        m_new = m_t[..., -1]
        decay = jnp.exp(m_inter[..., -1] - m_new)
        ws = jnp.exp(b[..., -1:] - b + lic - m_new[..., None])
        c_new = decay[..., None, None] * c_st + jnp.einsum('bhs,bhsk,bhsv->bhkv', ws, kc, vc)
        n_new = decay[..., None] * n_st + jnp.einsum('bhs,bhsk->bhk', ws, kc)
        return (c_new, n_new, m_new), h

    state, h = lax.scan(step, state, (chunk(q), chunk(k), chunk(v), chunk(li), chunk(lf)))
    h = jnp.moveaxis(h, 0, 2).reshape(bsz, nh, t, dh)
    return h, state


def _mlstm_final_state(k, v, li, lf):
    b = jnp.cumsum(lf, axis=-1)
    b_end = b[..., -1:]
    logw = b_end - b + li
    m = jnp.maximum(b_end[..., 0], jnp.max(logw, axis=-1))
    w = jnp.exp(logw - m[..., None])
    c_st = jnp.einsum('bhs,bhsk,bhsv->bhkv', w, k, v)
    n_st = jnp.einsum('bhs,bhsk->bhk', w, k)
    return (c_st, n_st, m)


def _mlstm_out(h, o, ml_norm, dtype):
    hn = h * lax.rsqrt(jnp.mean(h * h, axis=-1, keepdims=True) + EPS) * ml_norm[:, None, :].astype(jnp.float32)
    bsz, _, t, _ = h.shape
    hn = hn.transpose(0, 2, 1, 3).reshape(bsz, t, ML_W)
    return (o * hn).astype(dtype)


def _mlstm_bidir(p, b_gates, ml_norm, st_f, st_b):
    q, o = _mlstm_qo(p)
    k, v, (li_f, lf_f, li_b, lf_b) = _mlstm_kvg(p, b_gates)
    h_f, st_f_out = _mlstm_chunks(q, k, v, li_f, lf_f, st_f)
    h_b, st_b_out = _mlstm_chunks(_flip(q), _flip(k), _flip(v), _flip(li_b), _flip(lf_b), st_b)
    return _mlstm_out(h_f + _flip(h_b), o, ml_norm, p.dtype), st_f_out, st_b_out


def _short_conv(p, w_conv, grid):
    bsz, t, _ = p.shape
    gb = p[..., OFF_CV:OFF_CV + CV_W]
    gc = p[..., OFF_CV + CV_W:OFF_CV + 2 * CV_W]
    hin = p[..., OFF_CV + 2 * CV_W:OFF_CV + 3 * CV_W]
    z = gc * hin
    if grid:
        rows = t // GRID_W
        z = z.reshape(bsz, rows, GRID_W, CV_W)
    zp = jnp.pad(z, [(0, 0)] * (z.ndim - 2) + [(1, 1), (0, 0)])
    y = w_conv[0] * zp[..., :-2, :] + w_conv[1] * zp[..., 1:-1, :] + w_conv[2] * zp[..., 2:, :]
    return gb * y.reshape(bsz, t, CV_W)


def _chunk_mlp(p, gm_norm, w_s, b_s):
    bsz, t, _ = p.shape
    u = p[..., OFF_GM:OFF_GM + GM_W]
    v = _rmsnorm(p[..., OFF_GM + GM_W:OFF_GM + 2 * GM_W], gm_norm)
    v = v.reshape(bsz, t // GM_CHUNK, GM_CHUNK, GM_GROUPS, GM_W // GM_GROUPS)
    z = jnp.einsum('gts,bnsgc->bntgc', w_s, v) + b_s.T[:, :, None]
    return u * z.reshape(bsz, t, GM_W)


def setup_inputs(seed: int = 0) -> dict:
    key = jax.random.key(seed)
    ks = jax.random.split(key, 24)
    f32 = jnp.float32
    nrm = lambda k, shape, s: jax.random.normal(k, shape, f32) * s
    fbias = jnp.linspace(3.0, 6.0, ML_HEADS, dtype=f32)
    zb = jnp.zeros((ML_HEADS,), f32)
    gate_base = jnp.concatenate([zb, fbias, zb, fbias])
    return {
        "x": nrm(ks[0], (BATCH, SEQ, D_MODEL), 1.0),
        "c": nrm(ks[1], (BATCH, D_MODEL), 1.0),
        "ctx": nrm(ks[2], (BATCH, CTX_LEN, D_MODEL), 1.0),
        "c_ctx": nrm(ks[3], (D_MODEL,), 1.0),
        "w_ada": nrm(ks[4], (DEPTH, D_MODEL, 6 * D_MODEL), 0.5 * D_MODEL ** -0.5),
        "b_ada": nrm(ks[5], (DEPTH, 6 * D_MODEL), 0.02),
        "norm1": 1.0 + nrm(ks[6], (DEPTH, D_MODEL), 0.05),
        "norm2": 1.0 + nrm(ks[7], (DEPTH, D_MODEL), 0.05),
        "w_in": nrm(ks[8], (DEPTH, D_MODEL, D_IN), D_MODEL ** -0.5),
        "b_gates": gate_base + nrm(ks[9], (DEPTH, N_GATES), 0.1),
        "ml_norm": 1.0 + nrm(ks[10], (DEPTH, ML_HEADS, ML_DH), 0.05),
        "conv_w": nrm(ks[11], (DEPTH, CV_WIDTH, CV_W), CV_WIDTH ** -0.5),
        "gm_norm": 1.0 + nrm(ks[12], (DEPTH, GM_W), 0.05),
        "gm_ws": nrm(ks[13], (DEPTH, GM_GROUPS, GM_CHUNK, GM_CHUNK), GM_CHUNK ** -0.5),
        "gm_bs": 1.0 + nrm(ks[14], (DEPTH, GM_GROUPS, GM_CHUNK), 0.1),
        "w_out": nrm(ks[15], (DEPTH, D_MODEL, D_MODEL), D_MODEL ** -0.5),
        "w1": nrm(ks[16], (DEPTH, D_MODEL, D_FF), D_MODEL ** -0.5),
        "w3": nrm(ks[17], (DEPTH, D_MODEL, D_FF), D_MODEL ** -0.5),
        "w2": nrm(ks[18], (DEPTH, D_FF, D_MODEL), D_FF ** -0.5),
        "norm_f": 1.0 + nrm(ks[19], (D_MODEL,), 0.05),
    }


def reference(x, c, ctx, c_ctx, w_ada, b_ada, norm1, norm2, w_in, b_gates, ml_norm,
              conv_w, gm_norm, gm_ws, gm_bs, w_out, w1, w3, w2, norm_f):
    bsz = x.shape[0]
    for l in range(DEPTH):
        last = l == DEPTH - 1
        sh1, sc1, g1, sh2, sc2, g2 = [m[:, None, :] for m in _modulation(c, w_ada[l], b_ada[l])]
        csh1, csc1, cg1, csh2, csc2, cg2 = _modulation(c_ctx, w_ada[l], b_ada[l])
        hc = _modulate(_rmsnorm(ctx, norm1[l]), csh1, csc1)
        if last:
            pc = hc @ w_in[l][:, :OFF_Q]
            kc, vc, (lic_f, lfc_f, lic_b, lfc_b) = _mlstm_kvg(pc, b_gates[l])
            st_f = _mlstm_final_state(kc, vc, lic_f, lfc_f)
            st_b = _mlstm_final_state(_flip(kc), _flip(vc), _flip(lic_b), _flip(lfc_b))
        else:
            pc = hc @ w_in[l]
            zero = _zero_state(bsz)
            ml_c, st_f, st_b = _mlstm_bidir(pc, b_gates[l], ml_norm[l], zero, zero)
            mix_c = jnp.concatenate([ml_c,
                                     _short_conv(pc, conv_w[l], False),
                                     _chunk_mlp(pc, gm_norm[l], gm_ws[l], gm_bs[l])], axis=-1)
            ctx = ctx + cg1 * (mix_c @ w_out[l])
            hc2 = _modulate(_rmsnorm(ctx, norm2[l]), csh2, csc2)
            ctx = ctx + cg2 * _swiglu(hc2, w1[l], w3[l], w2[l])
        hx = _modulate(_rmsnorm(x, norm1[l]), sh1, sc1)
        px = hx @ w_in[l]
        ml_x, _, _ = _mlstm_bidir(px, b_gates[l], ml_norm[l], st_f, st_b)
        mix_x = jnp.concatenate([ml_x,
                                 _short_conv(px, conv_w[l], True),
                                 _chunk_mlp(px, gm_norm[l], gm_ws[l], gm_bs[l])], axis=-1)
        x = x + g1 * (mix_x @ w_out[l])
        hx2 = _modulate(_rmsnorm(x, norm2[l]), sh2, sc2)
        x = x + g2 * _swiglu(hx2, w1[l], w3[l], w2[l])
    return _rmsnorm(x, norm_f)
```

```python
import numpy as np
import concourse.bass as bass
import concourse.mybir as mybir
from concourse.bass_utils import run_bass_kernel_spmd

F32 = mybir.dt.float32
BF16 = mybir.dt.bfloat16
ALU = mybir.AluOpType
AF = mybir.ActivationFunctionType
AX = mybir.AxisListType

L = 4
D = 1024
KC = 8
T = 2048
TC = 256
DIN = 3344
DFF = 2816
FC = 22
NCORES = 8
EPS = 1e-6
ENG = ["pe", "act", "dve", "pool", "sp"]


class Prog:
    def __init__(self):
        self.q = {e: [] for e in ENG}
        self.tiles = {}
        self.seen = {e: {} for e in ENG}
        self.dma_n = {}
        self.tfin = {}
        self.efree = {e: 0.0 for e in ENG}
        self.pending = None

    def alias(self, new_tile, src_tile):
        if self.pending is not None:
            self.pending.append(("__alias__", new_tile, src_tile))
            return
        st = self.tiles.get(src_tile)
        if st is not None:
            self.tiles[new_tile] = [st[0], dict(st[1])]

    def _ready(self, eng, r, w):
        t = self.efree[eng]
        for tl in r:
            st = self.tiles.get(tl)
            if st is not None and st[0] is not None:
                t = max(t, self.tfin.get(st[0][:2], 0.0) + (0.05 if st[0][0] == eng else 0.2))
        for tl in w:
            st = self.tiles.get(tl)
            if st is not None:
                if st[0] is not None:
                    t = max(t, self.tfin.get(st[0][:2], 0.0) + (0.05 if st[0][0] == eng else 0.2))
                for tok in st[1].values():
                    t = max(t, self.tfin.get(tok[:2], 0.0) + (0.05 if tok[0] == eng else 0.2))
        return t

    def _advance(self, g):
        while True:
            self.pending = []
            try:
                next(g)
                grp = self.pending
            except StopIteration:
                grp = self.pending
                g = None
            self.pending = None
            if grp or g is None:
                return g, (grp if grp else None)

    def _head_time(self, grp):
        op = grp[0]
        if op[0] == "__alias__":
            return 0.0
        banks = [("BANK", t[1]) for t in list(op[2]) + list(op[3]) if isinstance(t, tuple) and t[0] == "PS"]
        return self._ready(op[0], op[2], list(op[3]) + banks)

    def _commit(self, grp):
        for op in grp:
            if op[0] == "__alias__":
                self.alias(op[1], op[2])
                continue
            self.add(*op[:4], dma=op[4], asyncw=op[5], dur=op[6])

    def sched(self, gens, bg=None, drain=False):
        heads = []
        for g in gens:
            if g is None:
                continue
            g2, grp = self._advance(g)
            if grp:
                heads.append([g2, grp])
        bgs = bg if bg is not None else []
        for b in bgs:
            if b[1] is None and b[0] is not None:
                b[0], b[1] = self._advance(b[0])
        rr_i = 0
        while heads or (drain and any(b[1] is not None for b in bgs)):
            best = None
            bt = None
            n = len(heads)
            for k in range(n):
                i = (rr_i + k) % n
                t = self._head_time(heads[i][1])
                if bt is None or t < bt - 1e-9:
                    bt = t
                    best = i
            bbest = None
            for b in bgs:
                if b[1] is None:
                    continue
                tb = self._head_time(b[1])
                if bt is None or tb < bt - 0.05:
                    bt = tb + 0.05 if bbest is None and heads else tb
                    bbest = b
            if bbest is not None:
                self._commit(bbest[1])
                bbest[1] = None
                if bbest[0] is not None:
                    bbest[0], bbest[1] = self._advance(bbest[0])
                continue
            g, grp = heads[best]
            self._commit(grp)
            rr_i = best + 1
            if g is None:
                heads.pop(best)
                continue
            g2, grp2 = self._advance(g)
            if grp2:
                heads[best] = [g2, grp2]
            else:
                heads.pop(best)

    def add(self, eng, fn, r=(), w=(), dma=None, asyncw=False, dur=0.3):
        if self.pending is not None:
            self.pending.append((eng, fn, tuple(r), tuple(w), dma, asyncw, dur))
            return
        need = {}
        banks = set()
        for t in list(r) + list(w):
            if isinstance(t, tuple) and t[0] == "PS":
                banks.add(("BANK", t[1]))
        if banks:
            w = list(w) + list(banks)

        def req(tok):
            if tok is None:
                return
            stream, p, af = tok
            if stream == eng and not af and eng == "pe":
                return
            if need.get(stream, -1) < p:
                need[stream] = p

        for t in r:
            st = self.tiles.get(t)
            if st is not None:
                req(st[0])
        for t in w:
            st = self.tiles.get(t)
            if st is not None:
                req(st[0])
                for tok in st[1].values():
                    req(tok)
        if dma is not None:
            n_prev = self.dma_n.get(dma, 0)
            if n_prev > 0:
                req((("dma", dma), n_prev - 1, True))
        deps = []
        seen = self.seen[eng]
        for stream, p in need.items():
            if seen.get(stream, -1) >= p:
                continue
            seen[stream] = p
            deps.append((stream, p))
        pos = len(self.q[eng])
        if dma is not None:
            n = self.dma_n.get(dma, 0)
            self.dma_n[dma] = n + 1
            tok = (("dma", dma), n, True)
        else:
            tok = (eng, pos, asyncw)
        self.q[eng].append([fn, deps, dma, False])
        t0 = self.efree[eng]
        for stream, p in need.items():
            t0 = max(t0, self.tfin.get((stream, p), 0.0) + (0.05 if stream == eng else 0.2))
        if dma is not None:
            self.efree[eng] = t0 + 0.1
            self.tfin[tok[:2]] = t0 + 2.0 + dur
        else:
            self.efree[eng] = t0 + dur
            self.tfin[tok[:2]] = t0 + dur
        for t in r:
            st = self.tiles.get(t)
            if st is None:
                st = [None, {}]
                self.tiles[t] = st
            st[1][tok[0]] = tok
        for t in w:
            self.tiles[t] = [tok, {}]

    def barrier(self):
        last = {}
        for e in ENG:
            for i in range(len(self.q[e]) - 1, -1, -1):
                if self.q[e][i][0] is not None and self.q[e][i][2] is None:
                    last[e] = i
                    break
        for k, n in self.dma_n.items():
            last[("dma", k)] = n - 1
        for e in ENG:
            deps = []
            for stream, p in last.items():
                if stream == e:
                    continue
                if self.seen[e].get(stream, -1) >= p:
                    continue
                self.seen[e][stream] = p
                deps.append((stream, p))
            self.q[e].append([None, deps, None, False])

    def prepare(self):
        for e in ENG:
            for op in self.q[e]:
                for stream, p in op[1]:
                    if isinstance(stream, str):
                        assert self.q[stream][p][0] is not None
                        self.q[stream][p][3] = True
        self.rank = {}
        for e in ENG:
            c = 0
            rk = []
            for op in self.q[e]:
                if op[3]:
                    c += 1
                rk.append(c)
            self.rank[e] = rk

    def emit_engine(self, e, eng, sems, dma_sems):
        rank = self.rank
        for fn, deps, dma, sig in self.q[e]:
            for stream, p in deps:
                if isinstance(stream, str):
                    eng.wait_ge(sems[stream], rank[stream][p])
                else:
                    eng.wait_ge(dma_sems[stream[1]], 16 * (p + 1))
            if fn is None:
                continue
            ins = fn(eng)
            if dma is not None:
                ins.then_inc(dma_sems[dma], 16)
            elif sig:
                ins.then_inc(sems[e], 1)


class Arena:
    def __init__(self, ap, nwords):
        self.ap = ap
        self.n = nwords
        self.off = 0
        self.log = []

    def _shape(self, v, shape):
        if len(shape) == 1:
            return v
        if len(shape) == 2:
            return v.rearrange("p (a b) -> p a b", b=shape[1])
        if len(shape) == 3:
            return v.rearrange("p (a b c) -> p a b c", b=shape[1], c=shape[2])
        raise ValueError

    def f32(self, *shape):
        n = int(np.prod(shape))
        assert self.off + n <= self.n, ("arena overflow", self.off, n, self.n)
        v = self.ap[:, self.off:self.off + n]
        out = self._shape(v, shape)
        self.log.append((id(out), self.off, "f32", shape, out))
        self.off += n
        return out

    def bf16(self, *shape):
        n = int(np.prod(shape))
        nw = (n + 1) // 2
        assert self.off + nw <= self.n, ("arena overflow", self.off, nw, self.n)
        v = self.ap[:, self.off:self.off + nw].bitcast(BF16)[:, 0:n]
        out = self._shape(v, shape)
        self.log.append((id(out), self.off, "bf16", shape, out))
        self.off += nw
        return out


class Rot:
    def __init__(self, n):
        self.n = n
        self.i = 0

    def next(self):
        i = self.i
        self.i = (i + 1) % self.n
        return i


STAGES = ["init", "prologue", "ctx_state", "ctx_main", "x_state", "x_main", "ffn"]


def build_program(n_layers=L, do_final=True, n_seq=2, stop_after="ffn", msec=99, dump_ctx=False, dump_arena=False):
    lvl = STAGES.index(stop_after)
    nc = bass.Bass("TRN2", target_bir_lowering=False)
    P = Prog()

    def din(name, shape):
        return nc.dram_tensor(name, list(shape), F32, kind="ExternalInput").ap()

    xT_d = din("xT", [2, D, T])
    ctxT_d = din("ctxT", [2, D, TC])
    cT_d = din("cT", [128, 24])
    wada_d = din("w_ada", [L, D, 6 * D])
    bada_d = din("bada", [128, L * 48])
    n1_d = din("n1", [128, L * 8])
    n2_d = din("n2", [128, L * 8])
    nf_d = din("nf", [128, 8])
    win_d = din("w_in", [L, D, DIN])
    bg_d = din("bg", [L, 128, 16])
    mln_d = din("mln", [L, 128, 512])
    cvw_d = din("cvw", [128, L * 6])
    gmn_d = din("gmn", [L, 128, 256])
    gmws_d = din("gmws", [L, 128, 512])
    gmb_d = din("gmb", [L, 128, 256])
    wout_d = din("w_out", [L, D, D])
    w1_d = din("w1", [L, D, DFF])
    w3_d = din("w3", [L, D, DFF])
    w2_d = din("w2", [L, DFF, D])
    out_d = nc.dram_tensor("outT", [2, D, T], F32, kind="ExternalOutput").ap()
    if dump_arena:
        dbg_d = nc.dram_tensor("dbgA", [128, 53200], F32, kind="ExternalOutput").ap()

    NW = 53200
    from contextlib import ExitStack
    with ExitStack() as es:
        arena_t = es.enter_context(nc.sbuf_tensor("arena", [128, NW], F32))
        psf = [es.enter_context(nc.psum_tensor("psf%d" % i, [128, 512], F32))[:] for i in range(7)]
        psb = es.enter_context(nc.psum_tensor("psb", [128, 1024], BF16))[:]
        sems = {e: es.enter_context(nc.semaphore("s_" + e)) for e in ENG}
        dma_sems = {}
        ld_rot = Rot(4)
        block = es.enter_context(nc.Block())

        A = Arena(arena_t[:], NW)

        def fsz(ap):
            n = 1
            for d in ap.shape[1:]:
                n *= int(d)
            return n

        def MM(out, lhsT, rhs, start, stop, r, w):
            f = 4.0 if rhs.dtype == F32 else 1.0
            P.add("pe", lambda e: e.matmul(out, lhsT=lhsT, rhs=rhs, start=start, stop=stop), r, w,
                  dur=f * (0.03 + max(fsz(rhs), 64) / 2000.0))

        def TR(out, in_, ident, r, w):
            P.add("pe", lambda e: e.transpose(out, in_=in_, identity=ident), r, w, dur=0.12)

        def ACT(out, in_, func, r, w, bias=None, scale=1.0, accum=None):
            def f(e):
                kw = dict(out=out, in_=in_, func=func, scale=scale)
                if bias is not None:
                    kw["bias"] = bias
                if accum is not None:
                    kw["accum_out"] = accum
                return e.activation(**kw)
            d = 0.12 + max(fsz(out), 64) / 1200.0
            if bias is not None or not isinstance(scale, float):
                d += 0.09
            if accum is not None:
                d += 0.1
            P.add("act", f, r, w, dur=d)

        def _vd(eng, out):
            if eng == "pool":
                return 0.4 + fsz(out) / 300.0
            return 0.08 + max(fsz(out), 64) / 960.0

        def TS(eng, out, in0, s1, s2, op0, op1, r, w):
            if op1 is None:
                P.add(eng, lambda e: e.tensor_scalar(out=out, in0=in0, scalar1=s1, scalar2=None, op0=op0), r, w, dur=_vd(eng, out))
            else:
                P.add(eng, lambda e: e.tensor_scalar(out=out, in0=in0, scalar1=s1, scalar2=s2, op0=op0, op1=op1), r, w,
                      dur=_vd(eng, out))

        def TT(eng, out, in0, in1, op, r, w):
            P.add(eng, lambda e: e.tensor_tensor(out=out, in0=in0, in1=in1, op=op), r, w, dur=_vd(eng, out))

        def STT(out, in0, scalar, in1, op0, op1, r, w):
            P.add("dve", lambda e: e.scalar_tensor_tensor(out=out, in0=in0, scalar=scalar, in1=in1, op0=op0, op1=op1), r, w,
                  dur=_vd("dve", out))

        def CP(eng, out, in_, r, w):
            P.add(eng, lambda e: e.tensor_copy(out=out, in_=in_), r, w, dur=_vd(eng, out))

        def RED(out, in_, op, r, w):
            P.add("dve", lambda e: e.tensor_reduce(out=out, in_=in_, axis=AX.X, op=op), r, w, dur=_vd("dve", in_))

        def RCP(out, in_, r, w):
            P.add("dve", lambda e: e.reciprocal(out=out, in_=in_), r, w, dur=_vd("dve", out))

        def MSET(eng, ap, val, w):
            P.add(eng, lambda e: e.memset(ap, val), (), w, asyncw=True, dur=_vd(eng, ap))

        def DMA(q, out, in_, key, r, w):
            P.add(q, lambda e: e.dma_start(out=out, in_=in_), r, w, dma=key, dur=fsz(out) * 128 * 4 / 150e3)

        xT = A.f32(KC, T)
        ctxT = A.f32(KC, TC)
        jm = A.f32(128)
        maskU = A.f32(128)
        maskL = A.f32(128)
        onesf = A.f32(128)
        identf = A.f32(128)
        identb = A.bf16(128)
        cst = A.f32(8)
        cT = A.f32(24)
        sc = A.f32(24)
        bada = A.f32(L * 48)
        n1 = A.f32(L * 8)
        n2 = A.f32(L * 8)
        nf = A.f32(8)
        cvw = A.f32(L * 6)
        mod = A.f32(L * 3, 48)
        G1 = A.f32(L * 3, 8)
        G2 = A.f32(L * 3, 8)
        acc = A.f32(512)
        rstd = A.f32(512)
        tmpn = [A.f32(512), A.f32(512)]
        sqb = tmpn
        zp = [A.f32(8 * 66)]
        zpc = [A.f32(258)]
        mark0 = A.off

        if dump_arena:
            for i in range(8):
                MSET("pool", arena_t[:, i * 6650:(i + 1) * 6650], 0.0, [("arena_init", i)])
            P.barrier()
        P.add("pool", lambda e: e.iota(jm, pattern=[[1, 128]], base=0, channel_multiplier=-1,
                                       allow_small_or_imprecise_dtypes=True), (), ["jm"])
        MSET("pool", onesf, 1.0, ["onesf"])
        MSET("pool", cst[:, 0:1], EPS, ["cst"])
        MSET("pool", cst[:, 1:2], 1.0, ["cst"])
        P.add("dve", lambda e: e.tensor_single_scalar(out=maskU, in_=jm, scalar=0.0, op=ALU.is_ge), ["jm"], ["maskU"])
        P.add("dve", lambda e: e.tensor_single_scalar(out=maskL, in_=jm, scalar=0.0, op=ALU.is_le), ["jm"], ["maskL"])
        P.add("dve", lambda e: e.tensor_single_scalar(out=identf, in_=jm, scalar=0.0, op=ALU.is_equal), ["jm"], ["identf"])
        CP("dve", identb, identf, ["identf"], ["identb"])
        for (dst, src, nm) in [(cT, cT_d, "cT"), (bada, bada_d, "bada"), (n1, n1_d, "n1"), (n2, n2_d, "n2"),
                               (nf, nf_d, "nf"), (cvw, cvw_d, "cvw")]:
            DMA("sp", dst, src, "ld%d" % ld_rot.next(), (), [nm])
        eps_c = cst[:, 0:1]
        one_c = cst[:, 1:2]

        m1 = A.off
        tmp24 = A.f32(24)
        ACT(tmp24, cT, AF.Exp, ["cT"], ["tmp24"], scale=-1.0)
        TS("dve", tmp24, tmp24, 1.0, None, ALU.add, None, ["tmp24"], ["tmp24"])
        RCP(tmp24, tmp24, ["tmp24"], ["tmp24"])
        TT("dve", sc, cT, tmp24, ALU.mult, ["cT", "tmp24"], ["sc"])
        wab = [A.bf16(KC, 512), A.bf16(KC, 512), A.bf16(KC, 512)]
        scb = A.bf16(24)
        CP("dve", scb, sc, ["sc"], ["scb"])
        modps = psf[0]
        wrot = Rot(3)
        for l in range(n_layers if lvl >= 1 else 0):
            for cg in range(12):
                bi = wrot.next()
                src = wada_d[l].rearrange("(kc p) c -> p kc c", p=128)[:, :, cg * 512:(cg + 1) * 512]
                DMA("pool", wab[bi], src, "wa%d" % bi, (), [("wab", bi)])
                for c4 in range(4):
                    ct = cg * 4 + c4
                    for kc in range(KC):
                        MM(modps[:, ct * 3:ct * 3 + 3], wab[bi][:, kc, c4 * 128:(c4 + 1) * 128], scb[:, kc * 3:kc * 3 + 3],
                           kc == 0, kc == KC - 1, [("wab", bi), "scb"], [("PS", 0, "mod", ct)])
            mps = modps[:, 0:144].rearrange("p (c b) -> p c b", b=3)
            for b in range(3):
                TT("dve", mod[:, l * 3 + b, :], mps[:, :, b], bada[:, l * 48:(l + 1) * 48], ALU.add,
                   [("PS", 0, "mod", ct) for ct in range(48)] + ["bada"], [("mod", l, b)])
                STT(G1[:, l * 3 + b, :], mod[:, l * 3 + b, 8:16], 1.0, n1[:, l * 8:(l + 1) * 8], ALU.add, ALU.mult,
                    [("mod", l, b), "n1"], [("G", l, b)])
                STT(G2[:, l * 3 + b, :], mod[:, l * 3 + b, 32:40], 1.0, n2[:, l * 8:(l + 1) * 8], ALU.add, ALU.mult,
                    [("mod", l, b), "n2"], [("G", l, b)])
        P.barrier()
        A.off = m1
        marks = []

        def mark(nm):
            marks.append((nm, max(P.efree.values())))
        mark("prologue")

        mark_ovl = A.off
        hTs = [A.bf16(KC, 512), A.bf16(KC, 512)]
        hT = hTs[0]
        wcg = [A.bf16(KC, 528), A.bf16(KC, 528)]
        qT = A.bf16(4, 512)
        kT = A.bf16(4, 512)
        ktok = A.bf16(4, 512)
        vtok = A.bf16(4, 512)
        osig = A.bf16(4, 512)
        mixT = A.bf16(KC, 512)
        cdb = A.bf16(16 * 4, 130)
        Cm = [A.f32(4, 130), A.f32(4, 130)]
        cdf = [A.bf16(4, 130), A.bf16(4, 130)]
        vext = [[A.bf16(4, 130), A.bf16(4, 130)], [A.bf16(4, 130), A.bf16(4, 130)]]
        Sfb = [A.bf16(4, 128), A.bf16(4, 128)]
        hs2 = [A.f32(512), A.f32(512)]
        otok2 = [A.bf16(512), A.bf16(512)]
        etmp = A.f32(512)
        go = etmp
        hbuf = A.f32(2, 512)
        ubuf = A.f32(2, 512)
        vn = [A.bf16(256), A.bf16(256)]
        junk = etmp[:, 0:256]
        gmtmp = [A.f32(128), A.f32(128)]
        gmws = A.bf16(512)
        gmb = A.f32(256)
        gmn = A.f32(256)
        mln = A.f32(512)
        bgb = A.f32(16)
        xg = [A.f32(16) for _ in range(8)]
        el = [A.f32(8) for _ in range(8)]
        ll = [A.f32(8) for _ in range(8)]
        aa = [A.f32(8) for _ in range(8)]
        AM = [A.f32(8) for _ in range(8)]
        BEN = [A.f32(8) for _ in range(8)]
        t12 = [A.f32(16) for _ in range(8)]
        wc = [A.f32(16) for _ in range(8)]
        amc = [A.f32(1) for _ in range(8)]
        dg = [A.f32(8) for _ in range(8)]
        mst = [A.f32(4), A.f32(4)]
        Ml = [A.f32(4) for _ in range(4)]
        dd = [A.f32(8) for _ in range(4)]
        ee = [A.f32(8) for _ in range(4)]
        wfc = [A.f32(8) for _ in range(4)]
        wfb_st = A.f32(16, 8)
        dden8 = [A.f32(8), A.f32(8)]
        rr8 = [A.f32(8), A.f32(8)]
        ssq2 = [A.f32(4), A.f32(4)]
        lnv2 = [A.f32(4), A.f32(4)]
        rsv2 = [A.f32(4), A.f32(4)]
        us_rot = Rot(3)
        gss = [A.f32(1), A.f32(1)]
        gl = [A.f32(1), A.f32(1)]
        gr = [A.f32(1), A.f32(1)]
        mark_main = A.off

        MSET("dve", zp[0], 0.0, [("zp", 0)])
        MSET("dve", zpc[0], 0.0, [("zpc", 0)])
        P.barrier()

        big_rot = Rot(3)
        st_rot = Rot(4)
        nd_rot = Rot(4)
        tp_rot = Rot(2)
        cg_rot = Rot(2)
        g_rot = Rot(4)

        def bigps():
            i = big_rot.next()
            return psf[i], ("PS", i, 0)

        scr = psf[3]
        stb = psf[4]

        u_rot = Rot(3)

        def ndslot():
            i = u_rot.next()
            return psf[i][:, 0:130], ("PS", i, "u")

        def modc(l, b, j):
            return mod[:, l * 3 + b, j * 8:(j + 1) * 8]

        def load_cg(l, c0, ncols, wd=None):
            bi = cg_rot.next()
            wd = win_d if wd is None else wd
            src = wd[l].rearrange("(kc p) c -> p kc c", p=128)[:, :, c0:c0 + ncols]
            DMA("pool", wcg[bi][:, :, 0:ncols], src, "cg%d" % bi, (), [("wcg", bi)])
            return wcg[bi], ("wcg", bi)

        def norm_g(res, rkey, nb, Gc, Sc, gkey, out_bf=None, out_key=None, inplace=False, psfn=None, acc_eng="dve"):
            for kc in range(KC):
                sb = sqb[kc % 2]
                ACT(sb[:, 0:nb], res[:, kc, :], AF.Square, [rkey(kc)], [("tmpn", kc % 2)])
                yield
                if kc == 0:
                    CP(acc_eng, acc[:, 0:nb], sb[:, 0:nb], [("tmpn", 0)], ["acc"])
                else:
                    TT(acc_eng, acc[:, 0:nb], acc[:, 0:nb], sb[:, 0:nb], ALU.add, [("tmpn", kc % 2), "acc"], ["acc"])
                yield
            ps, pk = (psfn or bigps)()
            MM(ps[:, 0:nb], onesf, acc[:, 0:nb], True, True, ["onesf", "acc"], [pk])
            yield
            ACT(rstd[:, 0:nb], ps[:, 0:nb], AF.Ln, [pk, "cst"], ["rstd"], bias=eps_c, scale=1.0 / D)
            yield
            ACT(rstd[:, 0:nb], rstd[:, 0:nb], AF.Exp, ["rstd"], ["rstd"], scale=-0.5)
            yield
            for kc in range(KC):
                if inplace:
                    STT(res[:, kc, :], res[:, kc, :], Gc[:, kc:kc + 1], rstd[:, 0:nb], ALU.mult, ALU.mult,
                        [rkey(kc), "rstd", gkey], [rkey(kc)])
                    yield
                else:
                    tb = tmpn[kc % 2]
                    STT(tb[:, 0:nb], res[:, kc, :], Gc[:, kc:kc + 1], rstd[:, 0:nb], ALU.mult, ALU.mult,
                        [rkey(kc), "rstd", gkey], [("tmpn", kc % 2)])
                    yield
                    ok_ = out_key(kc)
                    ACT(out_bf[:, kc, 0:nb], tb[:, 0:nb], AF.Identity, [("tmpn", kc % 2), gkey],
                        ok_ if isinstance(ok_, list) else [ok_], bias=Sc[:, kc:kc + 1])
                    yield

        def norm_block(*a, **k):
            for _ in norm_g(*a, **k):
                pass

        def proj_tok(w_ap, wkey, c0, ncols, ci, out_ps, okey, hp=0):
            for kc in range(KC):
                MM(out_ps, hTs[hp][:, kc, ci * 128:(ci + 1) * 128], w_ap[:, kc, c0:c0 + ncols], kc == 0, kc == KC - 1,
                   [("hT", hp, kc), wkey], [okey])

        def proj_feat(w_ap, wkey, c0, nb, out_ps, okey, hp=0):
            for kc in range(KC):
                MM(out_ps, w_ap[:, kc, c0:c0 + 128], hTs[hp][:, kc, 0:nb], kc == 0, kc == KC - 1,
                   [("hT", hp, kc), wkey], [okey])

        def zip_run(gens):
            gens = [g for g in gens if g is not None]
            while gens:
                for g in list(gens):
                    try:
                        next(g)
                    except StopIteration:
                        gens.remove(g)

        def gsc(ci):
            return psf[3 + ci % 3], 3 + ci % 3, (ci // 3) * 200

        def gate_math_g(ci, st=0):
            bank, bno, o = gsc(ci)
            gx = st * 4 + ci
            gps = bank[:, o:o + 16]
            bn = bank[:, o + 16:o + 24]
            ben = bank[:, o + 24:o + 32]
            amb = bank[:, o + 32:o + 40]
            aT = bank[0:8, o + 64:o + 192]
            _psn = ("gps", "bn", "ben", "amb", "aT")
            k = lambda n: ("PS", bno, n, ci) if n in _psn else (n, gx)
            TT("dve", xg[gx], gps, bgb, ALU.add, [k("gps"), "bgb"], [k("xg")])
            yield
            xg3 = xg[gx].rearrange("p (g h) -> p g h", h=4)
            el3 = el[gx].rearrange("p (g h) -> p g h", h=4)
            ACT(el3, xg3[:, 1::2, :], AF.Exp, [k("xg")], [k("el")], scale=-1.0)
            yield
            ACT(ll[gx], el[gx], AF.Ln, [k("el"), "cst"], [k("ll")], bias=one_c)
            yield
            MM(bn[:, 0:4], maskU, ll[gx][:, 0:4], True, True, ["maskU", k("ll")], [k("bn")])
            MM(bn[:, 4:8], maskL, ll[gx][:, 4:8], True, True, ["maskL", k("ll")], [k("bn")])
            MM(ben, onesf, ll[gx], True, True, ["onesf", k("ll")], [k("ben")])
            yield
            aa3 = aa[gx].rearrange("p (g h) -> p g h", h=4)
            bn3 = bn.rearrange("p (g h) -> p g h", h=4)
            TT("dve", aa3, xg3[:, 0::2, :], bn3, ALU.add, [k("xg"), k("bn")], [k("aa")])
            yield
            ACT(BEN[gx], ben, AF.Copy, [k("ben")], [k("BEN")])
            yield
            TR(aT, aa[gx], identf, [k("aa"), "identf"], [k("aT")])
            yield
            RED(amc[gx][0:8, :], aT, ALU.max, [k("aT")], [k("amc")])
            yield
            TS("dve", dg[gx][0:8, :], identf[0:8, 0:8], amc[gx][0:8, 0:1], None, ALU.mult, None,
               [k("amc"), "identf"], [k("dg")])
            yield
            MM(amb, onesf[0:8, :], dg[gx][0:8, :], True, True, ["onesf", k("dg")], [k("amb")])
            yield
            ACT(AM[gx], amb, AF.Copy, [k("amb")], [k("AM")])
            yield
            TT("dve", t12[gx][:, 0:8], aa[gx], AM[gx], ALU.subtract, [k("aa"), k("AM")], [k("t12")])
            TT("dve", t12[gx][:, 8:16], bn, AM[gx], ALU.subtract, [k("bn"), k("AM")], [k("t12")])
            yield
            ACT(wc[gx], t12[gx], AF.Exp, [k("t12")], [k("wc")])
            yield

        def recur(ci, d, st=0):
            gi = g_rot.next()
            cs = slice(d * 4, d * 4 + 4)
            ci = st * 4 + ci
            k = lambda n: (n, ci)
            g = lambda n: (n, gi)
            mk = ("mst", d)
            TT("dve", Ml[gi], mst[d], AM[ci][:, cs], ALU.max, [mk, k("AM")], [g("Ml")])
            TT("dve", dd[gi][:, 0:4], mst[d], Ml[gi], ALU.subtract, [mk, g("Ml")], [g("dd")])
            TT("dve", dd[gi][:, 4:8], AM[ci][:, cs], Ml[gi], ALU.subtract, [k("AM"), g("Ml")], [g("dd")])
            ACT(ee[gi], dd[gi], AF.Exp, [g("dd")], [g("ee")])
            TT("dve", mst[d], Ml[gi], BEN[ci][:, cs], ALU.subtract, [g("Ml"), k("BEN")], [mk])
            TT("dve", wfc[gi][:, 0:4], wc[ci][:, cs], ee[gi][:, 4:8], ALU.mult, [k("wc"), g("ee")], [g("wfc")])
            TT("dve", wfc[gi][:, 4:8], wc[ci][:, 8 + d * 4:8 + d * 4 + 4], ee[gi][:, 4:8], ALU.mult,
               [k("wc"), g("ee")], [g("wfc")])
            return gi

        def make_vext(ci, d, wf_ap, wfkey, vb):
            ve = vext[d][vb]
            vk = ("vext", d, vb)
            for h in range(4):
                ACT(ve[:, h, 0:128], vtok[:, ci, h * 128:(h + 1) * 128], AF.Copy, [("vtok", ci), wfkey], [vk],
                    scale=wf_ap[:, h:h + 1])
            CP("dve", ve[:, :, 128:129], wf_ap[:, 0:4].rearrange("p (h o) -> p h o", o=1), [wfkey], [vk])
            return ve, vk

        def state_update(ci, d, gi, ve, vk, cd_out=None, cd_key=None):
            for h in range(4):
                dec = ee[gi][:, h:h + 1]
                ck = ("Cm", d, h)
                if cd_out is not None:
                    ACT(cd_out(h), Cm[d][:, h, :], AF.Copy, [ck, ("ee", gi)], [cd_key(h)], scale=dec)
                ups, uk = ndslot()
                MM(ups[:, 0:129], ktok[:, ci, h * 128:(h + 1) * 128], ve[:, h, 0:129], True, True,
                   [("ktok", ci), vk], [uk])
                STT(Cm[d][:, h, 0:129], Cm[d][:, h, 0:129], dec, ups[:, 0:129], ALU.mult, ALU.add,
                    [ck, ("ee", gi), uk], [ck])

        KTs = [ktok, qT]
        VTs = [vtok, kT]

        def state_pass(l, res, rkey, tlen, mb, d, reuse=False):
            nb = min(512, tlen)
            nblk = tlen // nb
            ncb = nb // 128
            border = list(range(nblk)) if d == 0 else list(range(nblk - 1, -1, -1))

            def normp(bi, hp):
                t0 = bi * nb
                rk = lambda kc, bi=bi: rkey(bi, kc)
                yield from norm_g(res[:, :, t0:t0 + nb], rk, nb, G1[:, l * 3 + mb, :], modc(l, mb, 0), ("G", l, mb),
                                  out_bf=hTs[hp], out_key=lambda kc, hp=hp: ("hT", hp, kc))

            def prep(bi, st):
                hp = st
                w_ap, wk = load_cg(l, 0, 512)
                yield
                for ci in range(ncb):
                    ps, pk = bigps()
                    proj_tok(w_ap, wk, 0, 512, ci, ps, pk, hp)
                    yield
                    ACT(KTs[st][:, ci, :], ps, AF.Copy, [pk], [("kts", st, ci)], scale=128.0 ** -0.5)
                    yield
                w_ap, wk = load_cg(l, 512, 512)
                yield
                for ci in range(ncb):
                    ps, pk = bigps()
                    proj_tok(w_ap, wk, 0, 512, ci, ps, pk, hp)
                    yield
                    CP("dve", VTs[st][:, ci, :], ps, [pk], [("vts", st, ci)])
                    yield
                w_ap, wk = load_cg(l, 1024, 16)
                yield
                for ci in range(ncb):
                    bank, bno, o = gsc(ci)
                    proj_tok(w_ap, wk, 0, 16, ci, bank[:, o:o + 16], ("PS", bno, "gps", ci), hp)
                    yield
                gens = [gate_math_g(ci, st) for ci in range(ncb)]
                while gens:
                    for g in list(gens):
                        try:
                            next(g)
                            yield
                        except StopIteration:
                            gens.remove(g)

            def chain(bi, st):
                corder = range(ncb) if d == 0 else range(ncb - 1, -1, -1)
                cmk = [("Cm", d, h) for h in range(4)]
                for ci in corder:
                    j = bi * ncb + ci
                    par = j % 2
                    gi = recur(ci, d, st)
                    yield
                    if d == 1:
                        CP("dve", wfb_st[:, j, :], wfc[gi], [("wfc", gi)], [("wfb", j)])
                        yield
                    ve = vext[d][par]
                    vk = ("vext", d, par)
                    wf_ap = wfc[gi][:, 0:4]
                    TT("dve", ve[:, :, 0:128], VTs[st][:, ci, :].rearrange("p (h c) -> p h c", c=128),
                       wf_ap.unsqueeze(2).to_broadcast([128, 4, 128]), ALU.mult, [("vts", st, ci), ("wfc", gi)], [vk])
                    yield
                    CP("dve", ve[:, :, 128:129], wf_ap.rearrange("p (h o) -> p h o", o=1), [("wfc", gi)], [vk])
                    yield
                    TT("dve", Cm[d], Cm[d], ee[gi][:, 0:4].unsqueeze(2).to_broadcast([128, 4, 130]), ALU.mult,
                       cmk + [("ee", gi)], cmk)
                    yield
                    if d == 1:
                        ACT(cdb[:, j * 4:(j + 1) * 4, :], Cm[d], AF.Copy, cmk, [("cdb", j, h) for h in range(4)])
                        yield
                    for h in range(4):
                        if h < 3:
                            ups = psf[6][:, h * 129:(h + 1) * 129]
                            uk = ("PS", 6, "u")
                        else:
                            ups = psf[5][:, 256:385]
                            uk = ("PS", 5, "u3")
                        MM(ups, KTs[st][:, ci, h * 128:(h + 1) * 128], ve[:, h, 0:129], True, True,
                           [("kts", st, ci), vk], [uk])
                        yield
                    TT("dve", Cm[d][:, 0:3, 0:129], Cm[d][:, 0:3, 0:129],
                       psf[6][:, 0:387].rearrange("p (h c) -> p h c", c=129), ALU.add, cmk[0:3] + [("PS", 6, "u")], cmk[0:3])
                    yield
                    TT("dve", Cm[d][:, 3, 0:129], Cm[d][:, 3, 0:129], psf[5][:, 256:385], ALU.add,
                       [cmk[3], ("PS", 5, "u3")], [cmk[3]])
                    yield

            P.barrier()
            if not reuse:
                P.sched([normp(border[0], 0)])
                g0 = [prep(border[0], 0)]
                if len(border) > 1:
                    g0.append(normp(border[1], 1))
                P.sched(g0)
            for idx, bi in enumerate(border):
                gens = [chain(bi, idx % 2)]
                if idx + 1 < len(border):
                    gens.append(prep(border[idx + 1], (idx + 1) % 2))
                if idx + 2 < len(border):
                    gens.append(normp(border[idx + 2], idx % 2))
                P.sched(gens)
            P.barrier()

        bg_rot = Rot(2)

        def bgps():
            i = 4 + bg_rot.next()
            return psf[i], ("PS", i, "bg")

        def main_pass(l, res, rkey, tlen, mb, rl, hp0=None, reuse_kvg=False):
            nb = min(512, tlen)
            nblk = tlen // nb
            ncb = nb // 128
            rows = nb // rl
            Gc, Sc, gk = G1[:, l * 3 + mb, :], modc(l, mb, 0), ("G", l, mb)

            def normg(bi, hp, psfn=None):
                t0 = bi * nb
                return norm_g(res[:, :, t0:t0 + nb], (lambda kc, bi=bi: rkey(bi, kc)), nb, Gc, Sc, gk,
                              out_bf=hTs[hp], out_key=lambda kc, hp=hp: ("hT", hp, kc), psfn=psfn)

            P.barrier()
            if hp0 is None:
                hp0 = 0
                for _ in normg(0, 0):
                    pass
            for bi in range(nblk):
                t0 = bi * nb
                hp = (bi + hp0) % 2
                rk = lambda kc, bi=bi: rkey(bi, kc)
                w_ap, wk = load_cg(l, 0, 512)
                for ci in range(ncb):
                    if reuse_kvg:
                        break
                    ps, pk = bigps()
                    proj_tok(w_ap, wk, 0, 512, ci, ps, pk, hp)
                    ACT(ktok[:, ci, :], ps, AF.Copy, [pk], [("ktok", ci)], scale=128.0 ** -0.5)
                for h in range(4):
                    ps, pk = bigps()
                    proj_feat(w_ap, wk, h * 128, nb, ps[:, 0:nb], pk, hp)
                    ACT(kT[:, h, 0:nb], ps[:, 0:nb], AF.Copy, [pk], [("kT", h)], scale=128.0 ** -0.5)
                if not reuse_kvg:
                    w_ap, wk = load_cg(l, 512, 512)
                    for ci in range(ncb):
                        ps, pk = bigps()
                        proj_tok(w_ap, wk, 0, 512, ci, ps, pk, hp)
                        CP("dve", vtok[:, ci, :], ps, [pk], [("vtok", ci)])
                w_ap, wk = load_cg(l, 1024, 528)
                for ci in range(ncb):
                    if reuse_kvg:
                        break
                    bank, bno, o = gsc(ci)
                    proj_tok(w_ap, wk, 0, 16, ci, bank[:, o:o + 16], ("PS", bno, "gps", ci), hp)
                for h in range(4):
                    ps, pk = bigps()
                    proj_feat(w_ap, wk, 16 + h * 128, nb, ps[:, 0:nb], pk, hp)
                    CP("dve", qT[:, h, 0:nb], ps[:, 0:nb], [pk], [("qT", h)])
                w_ap, wk = load_cg(l, 1552, 512)
                for ci in range(ncb):
                    ps, pk = bigps()
                    proj_tok(w_ap, wk, 0, 512, ci, ps, pk, hp)
                    ACT(etmp, ps, AF.Exp, [pk], ["etmp"], scale=-1.0)
                    ACT(etmp, etmp, AF.Ln, ["etmp", "cst"], ["etmp"], bias=one_c)
                    ACT(osig[:, ci, :], etmp, AF.Exp, ["etmp"], [("osig", ci)], scale=-1.0)
                tq0 = max(P.efree.values())

                wout_cg = []

                def bg_gen(bi=bi, hp=hp):
                    w_ap, wk = load_cg(l, 2576, 512)
                    yield
                    for i in range(4):
                        ps, pk = bgps()
                        proj_feat(w_ap, wk, i * 128, nb, ps[:, 0:nb], pk, hp)
                        yield
                        if i < 2:
                            ACT(hbuf[:, i, 0:nb], ps[:, 0:nb], AF.Copy, [pk], [("hbuf", i)])
                        else:
                            ACT(ubuf[:, i - 2, 0:nb], ps[:, 0:nb], AF.Copy, [pk], [("ubuf", i - 2)])
                        yield
                    w_ap, wk = load_cg(l, 2064, 512)
                    yield
                    zsel, zkey = (zp, "zp") if rl == 64 else (zpc, "zpc")
                    for i in range(2):
                        ps, pk = bgps()
                        proj_feat(w_ap, wk, (2 + i) * 128, nb, ps[:, 0:nb], pk, hp)
                        yield
                        zv = zsel[0][:, 0:rows * (rl + 2)].rearrange("p (r c) -> p r c", c=rl + 2)
                        TT("dve", zv[:, :, 1:rl + 1], ps[:, 0:nb].rearrange("p (r c) -> p r c", c=rl),
                           hbuf[:, i, 0:nb].rearrange("p (r c) -> p r c", c=rl), ALU.mult, [pk, ("hbuf", i)], [(zkey, 0)])
                        yield
                        yv = hbuf[:, i, 0:nb].rearrange("p (r c) -> p r c", c=rl)
                        cw = lambda kk, i=i: cvw[:, l * 6 + i * 3 + kk:l * 6 + i * 3 + kk + 1]
                        TS("dve", yv, zv[:, :, 0:rl], cw(0), None, ALU.mult, None, [(zkey, 0), "cvw"], [("hbuf", i)])
                        yield
                        STT(yv, zv[:, :, 1:rl + 1], cw(1), yv, ALU.mult, ALU.add, [(zkey, 0), "cvw", ("hbuf", i)], [("hbuf", i)])
                        yield
                        STT(yv, zv[:, :, 2:rl + 2], cw(2), yv, ALU.mult, ALU.add, [(zkey, 0), "cvw", ("hbuf", i)], [("hbuf", i)])
                        yield
                    for i in range(2):
                        ps, pk = bgps()
                        proj_feat(w_ap, wk, i * 128, nb, ps[:, 0:nb], pk, hp)
                        yield
                        TT("dve", mixT[:, 4 + i, 0:nb], ps[:, 0:nb], hbuf[:, i, 0:nb], ALU.mult, [pk, ("hbuf", i)],
                           [("mix", 4 + i, c) for c in range(ncb)])
                        yield
                    w_ap, wk = load_cg(l, 3088, 256)
                    yield
                    for ci in range(ncb):
                        ps, pk = bgps()
                        proj_tok(w_ap, wk, 0, 256, ci, ps[:, 0:256], pk, hp)
                        yield
                        vb = ci % 2
                        ACT(vn[vb], ps[:, 0:256], AF.Square, [pk], [("vn", vb), ("gss", vb)], accum=gss[vb])
                        yield
                        ACT(gl[vb], gss[vb], AF.Ln, [("gss", vb), "cst"], [("gl", vb)], bias=eps_c, scale=1.0 / 256)
                        yield
                        ACT(gr[vb], gl[vb], AF.Exp, [("gl", vb)], [("gr", vb)], scale=-0.5)
                        yield
                        STT(vn[vb], ps[:, 0:256], gr[vb], gmn, ALU.mult, ALU.mult, [pk, ("gr", vb), "gmn"], [("vn", vb)])
                        yield
                        ps2, pk2 = bgps()
                        for i in range(2):
                            for hf in range(2):
                                g = 2 * i + hf
                                MM(ps2[64 * hf:64 * hf + 64, i * 128:(i + 1) * 128], vn[vb][:, g * 64:(g + 1) * 64],
                                   gmws[:, g * 128:(g + 1) * 128], True, True, [("vn", vb), "gmws"], [pk2])
                        yield
                        for i in range(2):
                            TT("dve", gmtmp[i], ps2[:, i * 128:(i + 1) * 128], gmb[:, i * 128:(i + 1) * 128], ALU.add,
                               [pk2, "gmb"], [("gmtmp", i)])
                            yield
                            TT("dve", mixT[:, 6 + i, ci * 128:(ci + 1) * 128], gmtmp[i], ubuf[:, i, ci * 128:(ci + 1) * 128],
                               ALU.mult, [("gmtmp", i), ("ubuf", i)], [("mix", 6 + i, ci)])
                            yield
                    wout_cg.append(load_cg(l, 0, 512, wd=wout_d))
                    yield
                    wout_cg.append(load_cg(l, 512, 512, wd=wout_d))
                    yield

                bg = [[bg_gen(), None]]
                if not reuse_kvg:
                    P.sched([gate_math_g(ci) for ci in range(ncb)])
                tq1 = max(P.efree.values())

                giF = {}

                def gen_S(ci):
                    j = bi * ncb + ci
                    par = j % 2
                    gi = recur(ci, 0)
                    giF[ci] = gi
                    yield
                    vt3 = vtok[:, ci, :].rearrange("p (h c) -> p h c", c=128)
                    for d, wf_ap, wfkey in ((0, wfc[gi][:, 0:4], ("wfc", gi)), (1, wfb_st[:, j, 0:4], ("wfb", j))):
                        ve = vext[d][par]
                        vk = ("vext", d, par)
                        TT("dve", ve[:, :, 0:128], vt3, wf_ap.unsqueeze(2).to_broadcast([128, 4, 128]), ALU.mult,
                           [("vtok", ci), wfkey], [vk])
                        yield
                        CP("dve", ve[:, :, 128:129], wf_ap.rearrange("p (h o) -> p h o", o=1), [wfkey], [vk])
                        yield
                    cmk = [("Cm", 0, h) for h in range(4)]
                    TT("dve", Cm[0], Cm[0], ee[gi][:, 0:4].unsqueeze(2).to_broadcast([128, 4, 130]), ALU.mult,
                       cmk + [("ee", gi)], cmk)
                    yield
                    cf = cdf[par]
                    ACT(cf, Cm[0], AF.Copy, cmk, [("cdf", par, h) for h in range(4)])
                    yield
                    vef = vext[0][par]
                    for h in range(4):
                        if h < 3:
                            ups = psf[6][:, h * 129:(h + 1) * 129]
                            uk = ("PS", 6, "u")
                        else:
                            ups = psf[1][:, 256:385]
                            uk = ("PS", 1, "u3")
                        MM(ups, ktok[:, ci, h * 128:(h + 1) * 128], vef[:, h, 0:129], True, True,
                           [("ktok", ci), ("vext", 0, par)], [uk])
                        yield
                    TT("dve", Cm[0][:, 0:3, 0:129], Cm[0][:, 0:3, 0:129],
                       psf[6][:, 0:387].rearrange("p (h c) -> p h c", c=129), ALU.add,
                       cmk[0:3] + [("PS", 6, "u")], cmk[0:3])
                    yield
                    TT("dve", Cm[0][:, 3, 0:129], Cm[0][:, 3, 0:129], psf[1][:, 256:385], ALU.add,
                       [cmk[3], ("PS", 1, "u3")], [cmk[3]])
                    yield

                def nd_ap(h):
                    b = psf[2 + h // 2]
                    o = (h % 2) * 256
                    return b[:, o:o + 128], b[:, o + 128:o + 256], ("PS", 2 + h // 2, "f", h % 2), ("PS", 2 + h // 2, "b", h % 2)

                def gen_H(ci, h):
                    j = bi * ncb + ci
                    par = j % 2
                    cb = ci * 128
                    vef, vkf = vext[0][par], ("vext", 0, par)
                    veb, vkb = vext[1][par], ("vext", 1, par)
                    cf = cdf[par]
                    sps = psf[h // 2][:, (h % 2) * 128:(h % 2 + 1) * 128]
                    sk = ("PS", h // 2, "st", h % 2)
                    MM(sps, kT[:, h, cb:cb + 128], qT[:, h, cb:cb + 128], True, True, [("kT", h), ("qT", h)], [sk])
                    yield
                    TT("dve", Sfb[0][:, h, :], sps, maskU, ALU.mult, [sk, "maskU"], [("Sf", h)])
                    yield
                    TT("dve", Sfb[1][:, h, :], sps, maskL, ALU.mult, [sk, "maskL"], [("Sb", h)])
                    yield
                    ndf, ndb, nkf, nkb = nd_ap(h)
                    dnf = psf[6][:, 400 + h:401 + h]
                    dnb = psf[6][:, 404 + h:405 + h]
                    dk = ("PS", 6, "den")
                    qc = qT[:, h, cb:cb + 128]
                    MM(ndf, Sfb[0][:, h, :], vef[:, h, 0:128], True, False, [("Sf", h), vkf], [nkf])
                    MM(ndf, qc, cf[:, h, 0:128], False, True, [("qT", h), ("cdf", par, h)], [nkf])
                    yield
                    MM(dnf, Sfb[0][:, h, :], vef[:, h, 128:129], True, False, [("Sf", h), vkf], [dk])
                    MM(dnf, qc, cf[:, h, 128:129], False, True, [("qT", h), ("cdf", par, h)], [dk])
                    yield
                    MM(ndb, Sfb[1][:, h, :], veb[:, h, 0:128], True, False, [("Sb", h), vkb], [nkb])
                    MM(ndb, qc, cdb[:, j * 4 + h, 0:128], False, True, [("qT", h), ("cdb", j, h)], [nkb])
                    yield
                    MM(dnb, Sfb[1][:, h, :], veb[:, h, 128:129], True, False, [("Sb", h), vkb], [dk])
                    MM(dnb, qc, cdb[:, j * 4 + h, 128:129], False, True, [("qT", h), ("cdb", j, h)], [dk])
                    yield

                def gen_T1(ci):
                    j = bi * ncb + ci
                    par = j % 2
                    gi = giF[ci]
                    dk = ("PS", 6, "den")
                    ACT(dden8[par], psf[6][:, 400:408], AF.Abs, [dk], [("dden", par)])
                    yield
                    TT("dve", dden8[par][:, 0:4], dden8[par][:, 0:4], wfc[gi][:, 4:8], ALU.max,
                       [("dden", par), ("wfc", gi)], [("dden", par)])
                    TT("dve", dden8[par][:, 4:8], dden8[par][:, 4:8], wfb_st[:, j, 4:8], ALU.max,
                       [("dden", par), ("wfb", j)], [("dden", par)])
                    yield
                    RCP(rr8[par], dden8[par], [("dden", par)], [("rr", par)])
                    yield
                    for h in range(4):
                        ndf, ndb, nkf, nkb = nd_ap(h)
                        hsl = hs2[par][:, h * 128:(h + 1) * 128]
                        hk = ("hs", par, h)
                        ACT(hsl, ndf, AF.Copy, [nkf, ("rr", par)], [hk], scale=rr8[par][:, h:h + 1])
                        yield
                        STT(hsl, ndb, rr8[par][:, 4 + h:5 + h], hsl, ALU.mult, ALU.add, [nkb, ("rr", par), hk], [hk])
                        yield

                def gen_T2(ci):
                    j = bi * ncb + ci
                    par = j % 2
                    cb = ci * 128
                    for h in range(4):
                        hsl = hs2[par][:, h * 128:(h + 1) * 128]
                        ACT(otok2[par][:, h * 128:(h + 1) * 128], hsl, AF.Square, [("hs", par, h)],
                            [("otok", par, h), ("ssq", par, h)], accum=ssq2[par][:, h:h + 1])
                        yield
                    ACT(lnv2[par], ssq2[par], AF.Ln, [("ssq", par, h) for h in range(4)] + ["cst"], [("lnv", par)],
                        bias=eps_c, scale=1.0 / 128)
                    yield
                    ACT(rsv2[par], lnv2[par], AF.Exp, [("lnv", par)], [("rsv", par)], scale=-0.5)
                    yield
                    TT("dve", go, osig[:, ci, :], mln, ALU.mult, [("osig", ci), "mln"], ["etmp"])
                    yield
                    for h in range(4):
                        STT(otok2[par][:, h * 128:(h + 1) * 128], hs2[par][:, h * 128:(h + 1) * 128], rsv2[par][:, h:h + 1],
                            go[:, h * 128:(h + 1) * 128], ALU.mult, ALU.mult, [("hs", par, h), ("rsv", par), "etmp"],
                            [("otok", par, h)])
                        yield
                    for h in range(4):
                        TR(psb[:, par * 512 + h * 128:par * 512 + (h + 1) * 128], otok2[par][:, h * 128:(h + 1) * 128], identb,
                           [("otok", par, h), "identb"], [("PS", 7, par)])
                        yield
                    ACT(mixT[:, 0:4, cb:cb + 128], psb[:, par * 512:(par + 1) * 512].rearrange("p (h t) -> p h t", t=128),
                        AF.Copy, [("PS", 7, par)], [("mix", h, ci) for h in range(4)])
                    yield

                P.sched([gen_S(0)], bg)
                for ci in range(ncb):
                    gens = [gen_H(ci, h) for h in range(4)]
                    if ci + 1 < ncb:
                        gens.append(gen_S(ci + 1))
                    if ci >= 1:
                        gens.append(gen_T2(ci - 1))
                    P.sched(gens, bg)
                    P.sched([gen_T1(ci)], bg)
                P.sched([gen_T2(ncb - 1)], bg)
                tq2 = max(P.efree.values())
                P.sched([], bg, drain=True)
                tq3 = max(P.efree.values())
                def wout_g():
                    for dt in range(KC):
                        w_ap, wk = wout_cg[dt // 4]
                        ps, pk = bigps()
                        for jj in range(KC):
                            MM(ps[:, 0:nb], w_ap[:, jj, (dt % 4) * 128:(dt % 4 + 1) * 128], mixT[:, jj, 0:nb], jj == 0, jj == KC - 1,
                               [wk] + [("mix", jj, c) for c in range(ncb)], [pk])
                        yield
                        STT(res[:, dt, t0:t0 + nb], ps[:, 0:nb], modc(l, mb, 2)[:, dt:dt + 1], res[:, dt, t0:t0 + nb],
                            ALU.mult, ALU.add, [pk, ("mod", l, mb), rk(dt)], [rk(dt)])
                        yield

                nbg = []
                if bi + 1 < nblk:
                    nbg = [[normg(bi + 1, 1 - hp, psfn=lambda: (psf[6], ("PS", 6, "ss"))), None]]
                P.sched([wout_g()], nbg)
                P.sched([], nbg, drain=True)
                tq4 = max(P.efree.values())
                if l == 0 and nblk == 4 and mb == 0:
                    print("main blk", bi, "pre_end %.0f gates %.0f chunks %.0f drain %.0f wout %.0f" % (tq0, tq1 - tq0, tq2 - tq1, tq3 - tq2, tq4 - tq3))
            P.barrier()

        def ffn(l, sbs):
            m = A.off
            A.off = mark_ovl
            tot = max(sum(p[2] for p in parts) for parts in sbs)
            hT2 = A.bf16(KC, tot)
            gT = A.bf16(FC, tot)
            w13 = [[A.bf16(KC, 256), A.bf16(KC, 256)] for _ in range(2)]
            w2b = [A.bf16(FC, 128), A.bf16(FC, 128)]
            stmp = [A.f32(512), A.f32(512)]
            wr = Rot(2)
            sr = Rot(2)

            def offsets(parts):
                offs, o = [], 0
                for p in parts:
                    offs.append(o)
                    o += p[2]
                return offs

            def gk(nm, a, o, n):
                return [(nm, a, g) for g in range(o // 256, (o + n) // 256)]

            def norms_g(parts):
                offs = offsets(parts)
                for (res, rk, n, mb), o in zip(parts, offs):
                    yield from norm_g(res, rk, n, G2[:, l * 3 + mb, :], modc(l, mb, 3), ("G", l, mb),
                                      out_bf=hT2[:, :, o:o + n], out_key=lambda kc, o=o, n=n: gk("hT2", kc, o, n))

            def f1_g(parts):
                offs = offsets(parts)
                for fg in range(FC // 2):
                    bi = wr.next()
                    for wi, wd in enumerate((w1_d, w3_d)):
                        src = wd[l].rearrange("(kc p) c -> p kc c", p=128)[:, :, fg * 256:(fg + 1) * 256]
                        DMA("pool", w13[bi][wi], src, "f%d%d" % (bi, wi), (), [("w13", bi, wi)])
                    yield
                    for f2 in range(2):
                        f = fg * 2 + f2
                        for pi, (res, rk, n, mb) in enumerate(parts):
                            o = offs[pi]
                            p1, k1 = bigps()
                            for kc in range(KC):
                                MM(p1[:, 0:n], w13[bi][0][:, kc, f2 * 128:(f2 + 1) * 128], hT2[:, kc, o:o + n], kc == 0,
                                   kc == KC - 1, [("w13", bi, 0)] + gk("hT2", kc, o, n), [k1])
                            yield
                            p3, k3 = bigps()
                            for kc in range(KC):
                                MM(p3[:, 0:n], w13[bi][1][:, kc, f2 * 128:(f2 + 1) * 128], hT2[:, kc, o:o + n], kc == 0,
                                   kc == KC - 1, [("w13", bi, 1)] + gk("hT2", kc, o, n), [k3])
                            yield
                            si = sr.next()
                            ACT(stmp[si][:, 0:n], p1[:, 0:n], AF.Silu, [k1], [("stmp", si)])
                            yield
                            TT("dve", gT[:, f, o:o + n], stmp[si][:, 0:n], p3[:, 0:n], ALU.mult, [("stmp", si), k3],
                               gk("gT", f, o, n))
                            yield

            def f2_g(parts):
                offs = offsets(parts)
                for dt in range(KC):
                    bi = wr.next()
                    src = w2_d[l].rearrange("(fc p) c -> p fc c", p=128)[:, :, dt * 128:(dt + 1) * 128]
                    DMA("pool", w2b[bi], src, "g%d" % bi, (), [("w2b", bi)])
                    yield
                    for pi, (res, rk, n, mb) in enumerate(parts):
                        o = offs[pi]
                        ps, pk = bigps()
                        for f in range(FC):
                            MM(ps[:, 0:n], w2b[bi][:, f, :], gT[:, f, o:o + n], f == 0, f == FC - 1,
                               [("w2b", bi)] + gk("gT", f, o, n), [pk])
                        yield
                        STT(res[:, dt, :], ps[:, 0:n], modc(l, mb, 5)[:, dt:dt + 1], res[:, dt, :], ALU.mult, ALU.add,
                            [pk, ("mod", l, mb), rk(dt)], [rk(dt)])
                        yield

            P.barrier()
            P.sched([norms_g(sbs[0])])
            P.sched([f1_g(sbs[0])])
            for k in range(len(sbs)):
                gens = [f2_g(sbs[k])]
                if k + 1 < len(sbs):
                    gens.append(norms_g(sbs[k + 1]))
                P.sched(gens)
                if k + 1 < len(sbs):
                    P.sched([f1_g(sbs[k + 1])])
            P.barrier()
            A.off = m

        xkey = lambda bi, kc: ("x", bi, kc)
        ckey = lambda bi, kc: ("c", bi, kc)
        for s in range(n_seq):
            for kc in range(KC):
                DMA("sp", xT[:, kc, :], xT_d[s, kc * 128:(kc + 1) * 128, :], "ld%d" % ld_rot.next(), (), [xkey(bi, kc) for bi in range(4)])
            DMA("sp", ctxT, ctxT_d[s].rearrange("(kc p) t -> p kc t", p=128), "ld%d" % ld_rot.next(), (), [ckey(0, kc) for kc in range(KC)])
            for l in range(n_layers):
                last = l == L - 1
                DMA("sp", bgb, bg_d[l], "ld%d" % ld_rot.next(), (), ["bgb"])
                DMA("sp", mln, mln_d[l], "ld%d" % ld_rot.next(), (), ["mln"])
                DMA("sp", gmn, gmn_d[l], "ld%d" % ld_rot.next(), (), ["gmn"])
                DMA("sp", gmb, gmb_d[l], "ld%d" % ld_rot.next(), (), ["gmb"])
                DMA("pool", gmws, gmws_d[l], "gm", (), ["gmws"])
                for d in range(2):
                    MSET("dve", Cm[d], 0.0, [("Cm", d, h) for h in range(4)])
                    MSET("dve", mst[d], 0.0, [("mst", d)])
                if lvl >= 2:
                    state_pass(l, ctxT, ckey, TC, 2, 1)
                mark("ctx_state")
                if not last:
                    if lvl >= 3:
                        main_pass(l, ctxT, ckey, TC, 2, TC, hp0=(0 if lvl >= 2 else None), reuse_kvg=(lvl >= 2))
                else:
                    state_pass(l, ctxT, ckey, TC, 2, 0, reuse=(lvl >= 2))
                mark("ctx_main")
                if lvl >= 4:
                    state_pass(l, xT, xkey, T, s, 1)
                mark("x_state")
                if lvl >= 5:
                    main_pass(l, xT, xkey, T, s, 64, hp0=(1 if lvl >= 4 else None))
                mark("x_main")
                P.barrier()
                if lvl < 6:
                    continue
                xparts = [(xT[:, :, bi * 512:(bi + 1) * 512], (lambda kc, bi=bi: xkey(bi, kc)), 512, s) for bi in range(4)]
                cpart = (ctxT, (lambda kc: ckey(0, kc)), TC, 2)
                if not last:
                    ffn(l, [[cpart] + xparts[0:2], xparts[2:4]])
                else:
                    ffn(l, [xparts[0:2], xparts[2:4]])
                mark("ffn")
            if do_final:
                for bi in range(4):
                    norm_block(xT[:, :, bi * 512:(bi + 1) * 512], (lambda kc, bi=bi: xkey(bi, kc)), 512, nf, None, "nf",
                               inplace=True)
            if dump_ctx:
                for kc in range(KC):
                    DMA("sp", out_d[s, kc * 128:(kc + 1) * 128, 0:TC], ctxT[:, kc, :], "o%d" % (kc % 4), [ckey(0, kc)], ())
            for kc in range(KC):
                if dump_ctx:
                    break
                DMA("sp", out_d[s, kc * 128:(kc + 1) * 128, :], xT[:, kc, :], "o%d" % (kc % 4), [xkey(bi, kc) for bi in range(4)], ())
            P.barrier()

        if dump_arena:
            P.barrier()
            for i in range(8):
                DMA("sp", dbg_d[:, i * 6650:(i + 1) * 6650], arena_t[:, i * 6650:(i + 1) * 6650], "o%d" % (i % 4), (), ())
            P.barrier()
            byid = {e[0]: e for e in A.log}
            names = {}

            def visit(nm, v):
                if isinstance(v, list):
                    for i, x in enumerate(v):
                        visit("%s.%d" % (nm, i), x)
                elif id(v) in byid:
                    e = byid[id(v)]
                    names[nm] = (e[1], e[2], tuple(int(x) for x in e[3]))
            for nm, v in list(locals().items()):
                visit(nm, v)
            nc._arena_names = names
        P.prepare()
        for k in P.dma_n:
            dma_sems[k] = es.enter_context(nc.semaphore("d_" + k))
        with nc.allow_low_precision(reason="bf16 matmul operands, fp32 accumulation"):
            @block.tensor
            def _(e):
                P.emit_engine("pe", e, sems, dma_sems)

            @block.scalar
            def _(e):
                P.emit_engine("act", e, sems, dma_sems)

            @block.vector
            def _(e):
                P.emit_engine("dve", e, sems, dma_sems)

            @block.gpsimd
            def _(e):
                P.emit_engine("pool", e, sems, dma_sems)

            @block.sync
            def _(e):
                P.emit_engine("sp", e, sems, dma_sems)
        prev = 0.0
        acc_t = {}
        for nm, t in marks:
            acc_t[nm] = acc_t.get(nm, 0.0) + (t - prev)
            prev = t
        print("phase est us:", {k: round(v) for k, v in acc_t.items()})
        print("ops:", {e: len(P.q[e]) for e in ENG}, "arena words:", A.off, "est_us:", {e: round(P.efree[e]) for e in ENG})
    return nc


def _host_inputs(inp):
    f = np.float32
    x = np.asarray(inp["x"], f)
    c = np.asarray(inp["c"], f)
    ctx = np.asarray(inp["ctx"], f)
    c_ctx = np.asarray(inp["c_ctx"], f)

    def cols(v):
        return np.ascontiguousarray(v.reshape(-1, 128).T)

    shared = {
        "w_ada": np.ascontiguousarray(inp["w_ada"], f),
        "bada": cols(np.asarray(inp["b_ada"], f)),
        "n1": cols(np.asarray(inp["norm1"], f)),
        "n2": cols(np.asarray(inp["norm2"], f)),
        "nf": cols(np.asarray(inp["norm_f"], f)),
        "w_in": np.ascontiguousarray(inp["w_in"], f),
        "bg": np.ascontiguousarray(np.broadcast_to(np.asarray(inp["b_gates"], f)[:, None, :], (L, 128, 16))),
        "mln": np.ascontiguousarray(np.broadcast_to(np.asarray(inp["ml_norm"], f).reshape(L, 1, 512), (L, 128, 512))),
        "cvw": np.ascontiguousarray(np.asarray(inp["conv_w"], f).reshape(L, 3, 2, 128).transpose(3, 0, 2, 1).reshape(128, L * 6)),
        "gmn": np.ascontiguousarray(np.broadcast_to(np.asarray(inp["gm_norm"], f)[:, None, :], (L, 128, 256))),
        "gmws": np.ascontiguousarray(np.asarray(inp["gm_ws"], f).transpose(0, 3, 1, 2).reshape(L, 128, 512)),
        "gmb": np.ascontiguousarray(
            np.repeat(np.asarray(inp["gm_bs"], f).reshape(L, 2, 2, 1, 128), 64, axis=3).reshape(L, 2, 128, 128)
            .transpose(0, 2, 1, 3).reshape(L, 128, 256)),
        "w_out": np.ascontiguousarray(inp["w_out"], f),
        "w1": np.ascontiguousarray(inp["w1"], f),
        "w3": np.ascontiguousarray(inp["w3"], f),
        "w2": np.ascontiguousarray(inp["w2"], f),
    }
    maps = []
    for i in range(NCORES):
        b0 = 2 * i
        cc = np.stack([c[b0], c[b0 + 1], c_ctx], axis=1)
        cT = np.ascontiguousarray(cc.reshape(8, 128, 3).transpose(1, 0, 2).reshape(128, 24))
        m = dict(shared)
        m["xT"] = np.ascontiguousarray(x[b0:b0 + 2].transpose(0, 2, 1))
        m["ctxT"] = np.ascontiguousarray(ctx[b0:b0 + 2].transpose(0, 2, 1))
        m["cT"] = cT
        maps.append(m)
    return maps


_NC_CACHE = {}


def kernel(**inputs):
    maps = _host_inputs(inputs)
    if "nc" not in _NC_CACHE:
        _NC_CACHE["nc"] = build_program()
    nc = _NC_CACHE["nc"]
    res = run_bass_kernel_spmd(nc, maps, core_ids=list(range(NCORES)))
    out = np.empty((2 * NCORES, T, D), np.float32)
    for i in range(NCORES):
        o = np.asarray(res.results[i]["outT"])
        out[2 * i:2 * i + 2] = o.transpose(0, 2, 1)
    return out
```

```python
import numpy as np
import concourse.bass as bass
import concourse.mybir as mybir
from concourse.bass_utils import run_bass_kernel_spmd

F32 = mybir.dt.float32
BF16 = mybir.dt.bfloat16
ALU = mybir.AluOpType
AF = mybir.ActivationFunctionType
AX = mybir.AxisListType

L = 4
D = 1024
KC = 8
T = 2048
TC = 256
DIN = 3344
DFF = 2816
FC = 22
NCORES = 8
EPS = 1e-6
ENG = ["pe", "act", "dve", "pool", "sp"]


class Prog:
    def __init__(self):
        self.q = {e: [] for e in ENG}
        self.tiles = {}
        self.seen = {e: {} for e in ENG}
        self.dma_n = {}
        self.tfin = {}
        self.efree = {e: 0.0 for e in ENG}
        self.pending = None

    def alias(self, new_tile, src_tile):
        if self.pending is not None:
            self.pending.append(("__alias__", new_tile, src_tile))
            return
        st = self.tiles.get(src_tile)
        if st is not None:
            self.tiles[new_tile] = [st[0], dict(st[1])]

    def _ready(self, eng, r, w):
        t = self.efree[eng]
        for tl in r:
            st = self.tiles.get(tl)
            if st is not None and st[0] is not None:
                t = max(t, self.tfin.get(st[0][:2], 0.0) + (0.05 if st[0][0] == eng else 0.2))
        for tl in w:
            st = self.tiles.get(tl)
            if st is not None:
                if st[0] is not None:
                    t = max(t, self.tfin.get(st[0][:2], 0.0) + (0.05 if st[0][0] == eng else 0.2))
                for tok in st[1].values():
                    t = max(t, self.tfin.get(tok[:2], 0.0) + (0.05 if tok[0] == eng else 0.2))
        return t

    def _advance(self, g):
        while True:
            self.pending = []
            try:
                next(g)
                grp = self.pending
            except StopIteration:
                grp = self.pending
                g = None
            self.pending = None
            if grp or g is None:
                return g, (grp if grp else None)

    def _head_time(self, grp):
        op = grp[0]
        if op[0] == "__alias__":
            return 0.0
        banks = [("BANK", t[1]) for t in list(op[2]) + list(op[3]) if isinstance(t, tuple) and t[0] == "PS"]
        return self._ready(op[0], op[2], list(op[3]) + banks)

    def _commit(self, grp):
        for op in grp:
            if op[0] == "__alias__":
                self.alias(op[1], op[2])
                continue
            self.add(*op[:4], dma=op[4], asyncw=op[5], dur=op[6])

    def sched(self, gens, bg=None, drain=False):
        heads = []
        for g in gens:
            if g is None:
                continue
            g2, grp = self._advance(g)
            if grp:
                heads.append([g2, grp])
        bgs = bg if bg is not None else []
        for b in bgs:
            if b[1] is None and b[0] is not None:
                b[0], b[1] = self._advance(b[0])
        rr_i = 0
        while heads or (drain and any(b[1] is not None for b in bgs)):
            best = None
            bt = None
            n = len(heads)
            for k in range(n):
                i = (rr_i + k) % n
                t = self._head_time(heads[i][1])
                if bt is None or t < bt - 1e-9:
                    bt = t
                    best = i
            bbest = None
            for b in bgs:
                if b[1] is None:
                    continue
                tb = self._head_time(b[1])
                if bt is None or tb < bt - 0.05:
                    bt = tb + 0.05 if bbest is None and heads else tb
                    bbest = b
            if bbest is not None:
                self._commit(bbest[1])
                bbest[1] = None
                if bbest[0] is not None:
                    bbest[0], bbest[1] = self._advance(bbest[0])
                continue
            g, grp = heads[best]
            self._commit(grp)
            rr_i = best + 1
            if g is None:
                heads.pop(best)
                continue
            g2, grp2 = self._advance(g)
            if grp2:
                heads[best] = [g2, grp2]
            else:
                heads.pop(best)

    def add(self, eng, fn, r=(), w=(), dma=None, asyncw=False, dur=0.3):
        if self.pending is not None:
            self.pending.append((eng, fn, tuple(r), tuple(w), dma, asyncw, dur))
            return
        need = {}
        banks = set()
        for t in list(r) + list(w):
            if isinstance(t, tuple) and t[0] == "PS":
                banks.add(("BANK", t[1]))
        if banks:
            w = list(w) + list(banks)

        def req(tok):
            if tok is None:
                return
            stream, p, af = tok
            if stream == eng and not af and eng == "pe":
                return
            if need.get(stream, -1) < p:
                need[stream] = p

        for t in r:
            st = self.tiles.get(t)
            if st is not None:
                req(st[0])
        for t in w:
            st = self.tiles.get(t)
            if st is not None:
                req(st[0])
                for tok in st[1].values():
                    req(tok)
        if dma is not None:
            n_prev = self.dma_n.get(dma, 0)
            if n_prev > 0:
                req((("dma", dma), n_prev - 1, True))
        deps = []
        seen = self.seen[eng]
        for stream, p in need.items():
            if seen.get(stream, -1) >= p:
                continue
            seen[stream] = p
            deps.append((stream, p))
        pos = len(self.q[eng])
        if dma is not None:
            n = self.dma_n.get(dma, 0)
            self.dma_n[dma] = n + 1
            tok = (("dma", dma), n, True)
        else:
            tok = (eng, pos, asyncw)
        self.q[eng].append([fn, deps, dma, False])
        t0 = self.efree[eng]
        for stream, p in need.items():
            t0 = max(t0, self.tfin.get((stream, p), 0.0) + (0.05 if stream == eng else 0.2))
        if dma is not None:
            self.efree[eng] = t0 + 0.1
            self.tfin[tok[:2]] = t0 + 2.0 + dur
        else:
            self.efree[eng] = t0 + dur
            self.tfin[tok[:2]] = t0 + dur
        for t in r:
            st = self.tiles.get(t)
            if st is None:
                st = [None, {}]
                self.tiles[t] = st
            st[1][tok[0]] = tok
        for t in w:
            self.tiles[t] = [tok, {}]

    def barrier(self):
        last = {}
        for e in ENG:
            for i in range(len(self.q[e]) - 1, -1, -1):
                if self.q[e][i][0] is not None and self.q[e][i][2] is None:
                    last[e] = i
                    break
        for k, n in self.dma_n.items():
            last[("dma", k)] = n - 1
        for e in ENG:
            deps = []
            for stream, p in last.items():
                if stream == e:
                    continue
                if self.seen[e].get(stream, -1) >= p:
                    continue
                self.seen[e][stream] = p
                deps.append((stream, p))
            self.q[e].append([None, deps, None, False])

    def prepare(self):
        for e in ENG:
            for op in self.q[e]:
                for stream, p in op[1]:
                    if isinstance(stream, str):
                        assert self.q[stream][p][0] is not None
                        self.q[stream][p][3] = True
        self.rank = {}
        for e in ENG:
            c = 0
            rk = []
            for op in self.q[e]:
                if op[3]:
                    c += 1
                rk.append(c)
            self.rank[e] = rk

    def emit_engine(self, e, eng, sems, dma_sems):
        rank = self.rank
        for fn, deps, dma, sig in self.q[e]:
            for stream, p in deps:
                if isinstance(stream, str):
                    eng.wait_ge(sems[stream], rank[stream][p])
                else:
                    eng.wait_ge(dma_sems[stream[1]], 16 * (p + 1))
            if fn is None:
                continue
            ins = fn(eng)
            if dma is not None:
                ins.then_inc(dma_sems[dma], 16)
            elif sig:
                ins.then_inc(sems[e], 1)


class Arena:
    def __init__(self, ap, nwords):
        self.ap = ap
        self.n = nwords
        self.off = 0
        self.log = []

    def _shape(self, v, shape):
        if len(shape) == 1:
            return v
        if len(shape) == 2:
            return v.rearrange("p (a b) -> p a b", b=shape[1])
        if len(shape) == 3:
            return v.rearrange("p (a b c) -> p a b c", b=shape[1], c=shape[2])
        raise ValueError

    def f32(self, *shape):
        n = int(np.prod(shape))
        assert self.off + n <= self.n, ("arena overflow", self.off, n, self.n)
        v = self.ap[:, self.off:self.off + n]
        out = self._shape(v, shape)
        self.log.append((id(out), self.off, "f32", shape, out))
        self.off += n
        return out

    def bf16(self, *shape):
        n = int(np.prod(shape))
        nw = (n + 1) // 2
        assert self.off + nw <= self.n, ("arena overflow", self.off, nw, self.n)
        v = self.ap[:, self.off:self.off + nw].bitcast(BF16)[:, 0:n]
        out = self._shape(v, shape)
        self.log.append((id(out), self.off, "bf16", shape, out))
        self.off += nw
        return out


class Rot:
    def __init__(self, n):
        self.n = n
        self.i = 0

    def next(self):
        i = self.i
        self.i = (i + 1) % self.n
        return i


STAGES = ["init", "prologue", "ctx_state", "ctx_main", "x_state", "x_main", "ffn"]


def build_program(n_layers=L, do_final=True, n_seq=2, stop_after="ffn", msec=99, dump_ctx=False, dump_arena=False):
    lvl = STAGES.index(stop_after)
    nc = bass.Bass("TRN2", target_bir_lowering=False)
    P = Prog()

    def din(name, shape):
        return nc.dram_tensor(name, list(shape), F32, kind="ExternalInput").ap()

    xT_d = din("xT", [2, D, T])
    ctxT_d = din("ctxT", [2, D, TC])
    cT_d = din("cT", [128, 24])
    wada_d = din("w_ada", [L, D, 6 * D])
    bada_d = din("bada", [128, L * 48])
    n1_d = din("n1", [128, L * 8])
    n2_d = din("n2", [128, L * 8])
    nf_d = din("nf", [128, 8])
    win_d = din("w_in", [L, D, DIN])
    bg_d = din("bg", [L, 128, 16])
    mln_d = din("mln", [L, 128, 512])
    cvw_d = din("cvw", [128, L * 6])
    gmn_d = din("gmn", [L, 128, 256])
    gmws_d = din("gmws", [L, 128, 512])
    gmb_d = din("gmb", [L, 128, 256])
    wout_d = din("w_out", [L, D, D])
    w1_d = din("w1", [L, D, DFF])
    w3_d = din("w3", [L, D, DFF])
    w2_d = din("w2", [L, DFF, D])
    out_d = nc.dram_tensor("outT", [2, D, T], F32, kind="ExternalOutput").ap()
    if dump_arena:
        dbg_d = nc.dram_tensor("dbgA", [128, 53200], F32, kind="ExternalOutput").ap()

    NW = 53200
    from contextlib import ExitStack
    with ExitStack() as es:
        arena_t = es.enter_context(nc.sbuf_tensor("arena", [128, NW], F32))
        psf = [es.enter_context(nc.psum_tensor("psf%d" % i, [128, 512], F32))[:] for i in range(7)]
        psb = es.enter_context(nc.psum_tensor("psb", [128, 1024], BF16))[:]
        sems = {e: es.enter_context(nc.semaphore("s_" + e)) for e in ENG}
        dma_sems = {}
        ld_rot = Rot(4)
        block = es.enter_context(nc.Block())

        A = Arena(arena_t[:], NW)

        def fsz(ap):
            n = 1
            for d in ap.shape[1:]:
                n *= int(d)
            return n

        def MM(out, lhsT, rhs, start, stop, r, w):
            f = 4.0 if rhs.dtype == F32 else 1.0
            P.add("pe", lambda e: e.matmul(out, lhsT=lhsT, rhs=rhs, start=start, stop=stop), r, w,
                  dur=f * (0.03 + max(fsz(rhs), 64) / 2000.0))

        def TR(out, in_, ident, r, w):
            P.add("pe", lambda e: e.transpose(out, in_=in_, identity=ident), r, w, dur=0.12)

        def ACT(out, in_, func, r, w, bias=None, scale=1.0, accum=None):
            def f(e):
                kw = dict(out=out, in_=in_, func=func, scale=scale)
                if bias is not None:
                    kw["bias"] = bias
                if accum is not None:
                    kw["accum_out"] = accum
                return e.activation(**kw)
            d = 0.12 + max(fsz(out), 64) / 1200.0
            if bias is not None or not isinstance(scale, float):
                d += 0.09
            if accum is not None:
                d += 0.1
            P.add("act", f, r, w, dur=d)

        def _vd(eng, out):
            if eng == "pool":
                return 0.4 + fsz(out) / 300.0
            return 0.08 + max(fsz(out), 64) / 960.0

        def TS(eng, out, in0, s1, s2, op0, op1, r, w):
            if op1 is None:
                P.add(eng, lambda e: e.tensor_scalar(out=out, in0=in0, scalar1=s1, scalar2=None, op0=op0), r, w, dur=_vd(eng, out))
            else:
                P.add(eng, lambda e: e.tensor_scalar(out=out, in0=in0, scalar1=s1, scalar2=s2, op0=op0, op1=op1), r, w,
                      dur=_vd(eng, out))

        def TT(eng, out, in0, in1, op, r, w):
            P.add(eng, lambda e: e.tensor_tensor(out=out, in0=in0, in1=in1, op=op), r, w, dur=_vd(eng, out))

        def STT(out, in0, scalar, in1, op0, op1, r, w):
            P.add("dve", lambda e: e.scalar_tensor_tensor(out=out, in0=in0, scalar=scalar, in1=in1, op0=op0, op1=op1), r, w,
                  dur=_vd("dve", out))

        def CP(eng, out, in_, r, w):
            P.add(eng, lambda e: e.tensor_copy(out=out, in_=in_), r, w, dur=_vd(eng, out))

        def RED(out, in_, op, r, w):
            P.add("dve", lambda e: e.tensor_reduce(out=out, in_=in_, axis=AX.X, op=op), r, w, dur=_vd("dve", in_))

        def RCP(out, in_, r, w):
            P.add("dve", lambda e: e.reciprocal(out=out, in_=in_), r, w, dur=_vd("dve", out))

        def MSET(eng, ap, val, w):
            P.add(eng, lambda e: e.memset(ap, val), (), w, asyncw=True, dur=_vd(eng, ap))

        def DMA(q, out, in_, key, r, w):
            P.add(q, lambda e: e.dma_start(out=out, in_=in_), r, w, dma=key, dur=fsz(out) * 128 * 4 / 150e3)

        xT = A.f32(KC, T)
        ctxT = A.f32(KC, TC)
        jm = A.f32(128)
        maskU = A.f32(128)
        maskL = A.f32(128)
        onesf = A.f32(128)
        identf = A.f32(128)
        identb = A.bf16(128)
        cst = A.f32(8)
        cT = A.f32(24)
        sc = A.f32(24)
        bada = A.f32(L * 48)
        n1 = A.f32(L * 8)
        n2 = A.f32(L * 8)
        nf = A.f32(8)
        cvw = A.f32(L * 6)
        mod = A.f32(L * 3, 48)
        G1 = A.f32(L * 3, 8)
        G2 = A.f32(L * 3, 8)
        acc = A.f32(512)
        rstd = A.f32(512)
        tmpn = [A.f32(512), A.f32(512)]
        sqb = tmpn
        zp = [A.f32(8 * 66)]
        zpc = [A.f32(258)]
        mark0 = A.off

        if dump_arena:
            for i in range(8):
                MSET("pool", arena_t[:, i * 6650:(i + 1) * 6650], 0.0, [("arena_init", i)])
            P.barrier()
        P.add("pool", lambda e: e.iota(jm, pattern=[[1, 128]], base=0, channel_multiplier=-1,
                                       allow_small_or_imprecise_dtypes=True), (), ["jm"])
        MSET("pool", onesf, 1.0, ["onesf"])
        MSET("pool", cst[:, 0:1], EPS, ["cst"])
        MSET("pool", cst[:, 1:2], 1.0, ["cst"])
        P.add("dve", lambda e: e.tensor_single_scalar(out=maskU, in_=jm, scalar=0.0, op=ALU.is_ge), ["jm"], ["maskU"])
        P.add("dve", lambda e: e.tensor_single_scalar(out=maskL, in_=jm, scalar=0.0, op=ALU.is_le), ["jm"], ["maskL"])
        P.add("dve", lambda e: e.tensor_single_scalar(out=identf, in_=jm, scalar=0.0, op=ALU.is_equal), ["jm"], ["identf"])
        CP("dve", identb, identf, ["identf"], ["identb"])
        for (dst, src, nm) in [(cT, cT_d, "cT"), (bada, bada_d, "bada"), (n1, n1_d, "n1"), (n2, n2_d, "n2"),
                               (nf, nf_d, "nf"), (cvw, cvw_d, "cvw")]:
            DMA("sp", dst, src, "ld%d" % ld_rot.next(), (), [nm])
        eps_c = cst[:, 0:1]
        one_c = cst[:, 1:2]

        m1 = A.off
        tmp24 = A.f32(24)
        ACT(tmp24, cT, AF.Exp, ["cT"], ["tmp24"], scale=-1.0)
        TS("dve", tmp24, tmp24, 1.0, None, ALU.add, None, ["tmp24"], ["tmp24"])
        RCP(tmp24, tmp24, ["tmp24"], ["tmp24"])
        TT("dve", sc, cT, tmp24, ALU.mult, ["cT", "tmp24"], ["sc"])
        wab = [A.bf16(KC, 512), A.bf16(KC, 512), A.bf16(KC, 512)]
        scb = A.bf16(24)
        CP("dve", scb, sc, ["sc"], ["scb"])
        modps = psf[0]
        wrot = Rot(3)
        for l in range(n_layers if lvl >= 1 else 0):
            for cg in range(12):
                bi = wrot.next()
                src = wada_d[l].rearrange("(kc p) c -> p kc c", p=128)[:, :, cg * 512:(cg + 1) * 512]
                DMA("pool", wab[bi], src, "wa%d" % bi, (), [("wab", bi)])
                for c4 in range(4):
                    ct = cg * 4 + c4
                    for kc in range(KC):
                        MM(modps[:, ct * 3:ct * 3 + 3], wab[bi][:, kc, c4 * 128:(c4 + 1) * 128], scb[:, kc * 3:kc * 3 + 3],
                           kc == 0, kc == KC - 1, [("wab", bi), "scb"], [("PS", 0, "mod", ct)])
            mps = modps[:, 0:144].rearrange("p (c b) -> p c b", b=3)
            for b in range(3):
                TT("dve", mod[:, l * 3 + b, :], mps[:, :, b], bada[:, l * 48:(l + 1) * 48], ALU.add,
                   [("PS", 0, "mod", ct) for ct in range(48)] + ["bada"], [("mod", l, b)])
                STT(G1[:, l * 3 + b, :], mod[:, l * 3 + b, 8:16], 1.0, n1[:, l * 8:(l + 1) * 8], ALU.add, ALU.mult,
                    [("mod", l, b), "n1"], [("G", l, b)])
                STT(G2[:, l * 3 + b, :], mod[:, l * 3 + b, 32:40], 1.0, n2[:, l * 8:(l + 1) * 8], ALU.add, ALU.mult,
                    [("mod", l, b), "n2"], [("G", l, b)])
        P.barrier()
        A.off = m1
        marks = []

        def mark(nm):
            marks.append((nm, max(P.efree.values())))
        mark("prologue")

        mark_ovl = A.off
        hTs = [A.bf16(KC, 512), A.bf16(KC, 512)]
        hT = hTs[0]
        wcg = [A.bf16(KC, 528), A.bf16(KC, 528)]
        qT = A.bf16(4, 512)
        kT = A.bf16(4, 512)
        ktok = A.bf16(4, 512)
        vtok = A.bf16(4, 512)
        osig = A.bf16(4, 512)
        mixT = A.bf16(KC, 512)
        cdb = A.bf16(16 * 4, 130)
        Cm = [A.f32(4, 130), A.f32(4, 130)]
        cdf = [A.bf16(4, 130), A.bf16(4, 130)]
        vext = [[A.bf16(4, 130), A.bf16(4, 130)], [A.bf16(4, 130), A.bf16(4, 130)]]
        Sfb = [A.bf16(4, 128), A.bf16(4, 128)]
        hs2 = [A.f32(512), A.f32(512)]
        otok2 = [A.bf16(512), A.bf16(512)]
        etmp = A.f32(512)
        go = etmp
        hbuf = A.f32(2, 512)
        ubuf = A.f32(2, 512)
        vn = [A.bf16(256), A.bf16(256)]
        junk = etmp[:, 0:256]
        gmtmp = [A.f32(128), A.f32(128)]
        gmws = A.bf16(512)
        gmb = A.f32(256)
        gmn = A.f32(256)
        mln = A.f32(512)
        bgb = A.f32(16)
        xg = [A.f32(16) for _ in range(8)]
        el = [A.f32(8) for _ in range(8)]
        ll = [A.f32(8) for _ in range(8)]
        aa = [A.f32(8) for _ in range(8)]
        AM = [A.f32(8) for _ in range(8)]
        BEN = [A.f32(8) for _ in range(8)]
        t12 = [A.f32(16) for _ in range(8)]
        wc = [A.f32(16) for _ in range(8)]
        amc = [A.f32(1) for _ in range(8)]
        dg = [A.f32(8) for _ in range(8)]
        mst = [A.f32(4), A.f32(4)]
        Ml = [A.f32(4) for _ in range(4)]
        dd = [A.f32(8) for _ in range(4)]
        ee = [A.f32(8) for _ in range(4)]
        wfc = [A.f32(8) for _ in range(4)]
        wfb_st = A.f32(16, 8)
        dden8 = [A.f32(8), A.f32(8)]
        rr8 = [A.f32(8), A.f32(8)]
        ssq2 = [A.f32(4), A.f32(4)]
        lnv2 = [A.f32(4), A.f32(4)]
        rsv2 = [A.f32(4), A.f32(4)]
        us_rot = Rot(3)
        gss = [A.f32(1), A.f32(1)]
        gl = [A.f32(1), A.f32(1)]
        gr = [A.f32(1), A.f32(1)]
        mark_main = A.off

        MSET("dve", zp[0], 0.0, [("zp", 0)])
        MSET("dve", zpc[0], 0.0, [("zpc", 0)])
        P.barrier()

        big_rot = Rot(3)
        st_rot = Rot(4)
        nd_rot = Rot(4)
        tp_rot = Rot(2)
        cg_rot = Rot(2)
        g_rot = Rot(4)

        def bigps():
            i = big_rot.next()
            return psf[i], ("PS", i, 0)

        scr = psf[3]
        stb = psf[4]

        u_rot = Rot(3)

        def ndslot():
            i = u_rot.next()
            return psf[i][:, 0:130], ("PS", i, "u")

        def modc(l, b, j):
            return mod[:, l * 3 + b, j * 8:(j + 1) * 8]

        def load_cg(l, c0, ncols, wd=None):
            bi = cg_rot.next()
            wd = win_d if wd is None else wd
            src = wd[l].rearrange("(kc p) c -> p kc c", p=128)[:, :, c0:c0 + ncols]
            DMA("pool", wcg[bi][:, :, 0:ncols], src, "cg%d" % bi, (), [("wcg", bi)])
            return wcg[bi], ("wcg", bi)

        def norm_g(res, rkey, nb, Gc, Sc, gkey, out_bf=None, out_key=None, inplace=False, psfn=None, acc_eng="dve"):
            for kc in range(KC):
                sb = sqb[kc % 2]
                ACT(sb[:, 0:nb], res[:, kc, :], AF.Square, [rkey(kc)], [("tmpn", kc % 2)])
                yield
                if kc == 0:
                    CP(acc_eng, acc[:, 0:nb], sb[:, 0:nb], [("tmpn", 0)], ["acc"])
                else:
                    TT(acc_eng, acc[:, 0:nb], acc[:, 0:nb], sb[:, 0:nb], ALU.add, [("tmpn", kc % 2), "acc"], ["acc"])
                yield
            ps, pk = (psfn or bigps)()
            MM(ps[:, 0:nb], onesf, acc[:, 0:nb], True, True, ["onesf", "acc"], [pk])
            yield
            ACT(rstd[:, 0:nb], ps[:, 0:nb], AF.Ln, [pk, "cst"], ["rstd"], bias=eps_c, scale=1.0 / D)
            yield
            ACT(rstd[:, 0:nb], rstd[:, 0:nb], AF.Exp, ["rstd"], ["rstd"], scale=-0.5)
            yield
            for kc in range(KC):
                if inplace:
                    STT(res[:, kc, :], res[:, kc, :], Gc[:, kc:kc + 1], rstd[:, 0:nb], ALU.mult, ALU.mult,
                        [rkey(kc), "rstd", gkey], [rkey(kc)])
                    yield
                else:
                    tb = tmpn[kc % 2]
                    STT(tb[:, 0:nb], res[:, kc, :], Gc[:, kc:kc + 1], rstd[:, 0:nb], ALU.mult, ALU.mult,
                        [rkey(kc), "rstd", gkey], [("tmpn", kc % 2)])
                    yield
                    ok_ = out_key(kc)
                    ACT(out_bf[:, kc, 0:nb], tb[:, 0:nb], AF.Identity, [("tmpn", kc % 2), gkey],
                        ok_ if isinstance(ok_, list) else [ok_], bias=Sc[:, kc:kc + 1])
                    yield

        def norm_block(*a, **k):
            for _ in norm_g(*a, **k):
                pass

        def proj_tok(w_ap, wkey, c0, ncols, ci, out_ps, okey, hp=0):
            for kc in range(KC):
                MM(out_ps, hTs[hp][:, kc, ci * 128:(ci + 1) * 128], w_ap[:, kc, c0:c0 + ncols], kc == 0, kc == KC - 1,
                   [("hT", hp, kc), wkey], [okey])

        def proj_feat(w_ap, wkey, c0, nb, out_ps, okey, hp=0):
            for kc in range(KC):
                MM(out_ps, w_ap[:, kc, c0:c0 + 128], hTs[hp][:, kc, 0:nb], kc == 0, kc == KC - 1,
                   [("hT", hp, kc), wkey], [okey])

        def zip_run(gens):
            gens = [g for g in gens if g is not None]
            while gens:
                for g in list(gens):
                    try:
                        next(g)
                    except StopIteration:
                        gens.remove(g)

        def gsc(ci):
            return psf[3 + ci % 3], 3 + ci % 3, (ci // 3) * 200

        def gate_math_g(ci, st=0):
            bank, bno, o = gsc(ci)
            gx = st * 4 + ci
            gps = bank[:, o:o + 16]
            bn = bank[:, o + 16:o + 24]
            ben = bank[:, o + 24:o + 32]
            amb = bank[:, o + 32:o + 40]
            aT = bank[0:8, o + 64:o + 192]
            _psn = ("gps", "bn", "ben", "amb", "aT")
            k = lambda n: ("PS", bno, n, ci) if n in _psn else (n, gx)
            TT("dve", xg[gx], gps, bgb, ALU.add, [k("gps"), "bgb"], [k("xg")])
            yield
            xg3 = xg[gx].rearrange("p (g h) -> p g h", h=4)
            el3 = el[gx].rearrange("p (g h) -> p g h", h=4)
            ACT(el3, xg3[:, 1::2, :], AF.Exp, [k("xg")], [k("el")], scale=-1.0)
            yield
            ACT(ll[gx], el[gx], AF.Ln, [k("el"), "cst"], [k("ll")], bias=one_c)
            yield
            MM(bn[:, 0:4], maskU, ll[gx][:, 0:4], True, True, ["maskU", k("ll")], [k("bn")])
            MM(bn[:, 4:8], maskL, ll[gx][:, 4:8], True, True, ["maskL", k("ll")], [k("bn")])
            MM(ben, onesf, ll[gx], True, True, ["onesf", k("ll")], [k("ben")])
            yield
            aa3 = aa[gx].rearrange("p (g h) -> p g h", h=4)
            bn3 = bn.rearrange("p (g h) -> p g h", h=4)
            TT("dve", aa3, xg3[:, 0::2, :], bn3, ALU.add, [k("xg"), k("bn")], [k("aa")])
            yield
            ACT(BEN[gx], ben, AF.Copy, [k("ben")], [k("BEN")])
            yield
            TR(aT, aa[gx], identf, [k("aa"), "identf"], [k("aT")])
            yield
            RED(amc[gx][0:8, :], aT, ALU.max, [k("aT")], [k("amc")])
            yield
            TS("dve", dg[gx][0:8, :], identf[0:8, 0:8], amc[gx][0:8, 0:1], None, ALU.mult, None,
               [k("amc"), "identf"], [k("dg")])
            yield
            MM(amb, onesf[0:8, :], dg[gx][0:8, :], True, True, ["onesf", k("dg")], [k("amb")])
            yield
            ACT(AM[gx], amb, AF.Copy, [k("amb")], [k("AM")])
            yield
            TT("dve", t12[gx][:, 0:8], aa[gx], AM[gx], ALU.subtract, [k("aa"), k("AM")], [k("t12")])
            TT("dve", t12[gx][:, 8:16], bn, AM[gx], ALU.subtract, [k("bn"), k("AM")], [k("t12")])
            yield
            ACT(wc[gx], t12[gx], AF.Exp, [k("t12")], [k("wc")])
            yield

        def recur(ci, d, st=0):
            gi = g_rot.next()
            cs = slice(d * 4, d * 4 + 4)
            ci = st * 4 + ci
            k = lambda n: (n, ci)
            g = lambda n: (n, gi)
            mk = ("mst", d)
            TT("dve", Ml[gi], mst[d], AM[ci][:, cs], ALU.max, [mk, k("AM")], [g("Ml")])
            TT("dve", dd[gi][:, 0:4], mst[d], Ml[gi], ALU.subtract, [mk, g("Ml")], [g("dd")])
            TT("dve", dd[gi][:, 4:8], AM[ci][:, cs], Ml[gi], ALU.subtract, [k("AM"), g("Ml")], [g("dd")])
            ACT(ee[gi], dd[gi], AF.Exp, [g("dd")], [g("ee")])
            TT("dve", mst[d], Ml[gi], BEN[ci][:, cs], ALU.subtract, [g("Ml"), k("BEN")], [mk])
            TT("dve", wfc[gi][:, 0:4], wc[ci][:, cs], ee[gi][:, 4:8], ALU.mult, [k("wc"), g("ee")], [g("wfc")])
            TT("dve", wfc[gi][:, 4:8], wc[ci][:, 8 + d * 4:8 + d * 4 + 4], ee[gi][:, 4:8], ALU.mult,
               [k("wc"), g("ee")], [g("wfc")])
            return gi

        def make_vext(ci, d, wf_ap, wfkey, vb):
            ve = vext[d][vb]
            vk = ("vext", d, vb)
            for h in range(4):
                ACT(ve[:, h, 0:128], vtok[:, ci, h * 128:(h + 1) * 128], AF.Copy, [("vtok", ci), wfkey], [vk],
                    scale=wf_ap[:, h:h + 1])
            CP("dve", ve[:, :, 128:129], wf_ap[:, 0:4].rearrange("p (h o) -> p h o", o=1), [wfkey], [vk])
            return ve, vk

        def state_update(ci, d, gi, ve, vk, cd_out=None, cd_key=None):
            for h in range(4):
                dec = ee[gi][:, h:h + 1]
                ck = ("Cm", d, h)
                if cd_out is not None:
                    ACT(cd_out(h), Cm[d][:, h, :], AF.Copy, [ck, ("ee", gi)], [cd_key(h)], scale=dec)
                ups, uk = ndslot()
                MM(ups[:, 0:129], ktok[:, ci, h * 128:(h + 1) * 128], ve[:, h, 0:129], True, True,
                   [("ktok", ci), vk], [uk])
                STT(Cm[d][:, h, 0:129], Cm[d][:, h, 0:129], dec, ups[:, 0:129], ALU.mult, ALU.add,
                    [ck, ("ee", gi), uk], [ck])

        KTs = [ktok, qT]
        VTs = [vtok, kT]

        def state_pass(l, res, rkey, tlen, mb, d, reuse=False):
            nb = min(512, tlen)
            nblk = tlen // nb
            ncb = nb // 128
            border = list(range(nblk)) if d == 0 else list(range(nblk - 1, -1, -1))

            def normp(bi, hp):
                t0 = bi * nb
                rk = lambda kc, bi=bi: rkey(bi, kc)
                yield from norm_g(res[:, :, t0:t0 + nb], rk, nb, G1[:, l * 3 + mb, :], modc(l, mb, 0), ("G", l, mb),
                                  out_bf=hTs[hp], out_key=lambda kc, hp=hp: ("hT", hp, kc))

            def prep(bi, st):
                hp = st
                w_ap, wk = load_cg(l, 0, 512)
                yield
                for ci in range(ncb):
                    ps, pk = bigps()
                    proj_tok(w_ap, wk, 0, 512, ci, ps, pk, hp)
                    yield
                    ACT(KTs[st][:, ci, :], ps, AF.Copy, [pk], [("kts", st, ci)], scale=128.0 ** -0.5)
                    yield
                w_ap, wk = load_cg(l, 512, 512)
                yield
                for ci in range(ncb):
                    ps, pk = bigps()
                    proj_tok(w_ap, wk, 0, 512, ci, ps, pk, hp)
                    yield
                    CP("dve", VTs[st][:, ci, :], ps, [pk], [("vts", st, ci)])
                    yield
                w_ap, wk = load_cg(l, 1024, 16)
                yield
                for ci in range(ncb):
                    bank, bno, o = gsc(ci)
                    proj_tok(w_ap, wk, 0, 16, ci, bank[:, o:o + 16], ("PS", bno, "gps", ci), hp)
                    yield
                gens = [gate_math_g(ci, st) for ci in range(ncb)]
                while gens:
                    for g in list(gens):
                        try:
                            next(g)
                            yield
                        except StopIteration:
                            gens.remove(g)

            def chain(bi, st):
                corder = range(ncb) if d == 0 else range(ncb - 1, -1, -1)
                cmk = [("Cm", d, h) for h in range(4)]
                for ci in corder:
                    j = bi * ncb + ci
                    par = j % 2
                    gi = recur(ci, d, st)
                    yield
                    if d == 1:
                        CP("dve", wfb_st[:, j, :], wfc[gi], [("wfc", gi)], [("wfb", j)])
                        yield
                    ve = vext[d][par]
                    vk = ("vext", d, par)
                    wf_ap = wfc[gi][:, 0:4]
                    TT("dve", ve[:, :, 0:128], VTs[st][:, ci, :].rearrange("p (h c) -> p h c", c=128),
                       wf_ap.unsqueeze(2).to_broadcast([128, 4, 128]), ALU.mult, [("vts", st, ci), ("wfc", gi)], [vk])
                    yield
                    CP("dve", ve[:, :, 128:129], wf_ap.rearrange("p (h o) -> p h o", o=1), [("wfc", gi)], [vk])
                    yield
                    TT("dve", Cm[d], Cm[d], ee[gi][:, 0:4].unsqueeze(2).to_broadcast([128, 4, 130]), ALU.mult,
                       cmk + [("ee", gi)], cmk)
                    yield
                    if d == 1:
                        ACT(cdb[:, j * 4:(j + 1) * 4, :], Cm[d], AF.Copy, cmk, [("cdb", j, h) for h in range(4)])
                        yield
                    for h in range(4):
                        if h < 3:
                            ups = psf[6][:, h * 129:(h + 1) * 129]
                            uk = ("PS", 6, "u")
                        else:
                            ups = psf[5][:, 256:385]
                            uk = ("PS", 5, "u3")
                        MM(ups, KTs[st][:, ci, h * 128:(h + 1) * 128], ve[:, h, 0:129], True, True,
                           [("kts", st, ci), vk], [uk])
                        yield
                    TT("dve", Cm[d][:, 0:3, 0:129], Cm[d][:, 0:3, 0:129],
                       psf[6][:, 0:387].rearrange("p (h c) -> p h c", c=129), ALU.add, cmk[0:3] + [("PS", 6, "u")], cmk[0:3])
                    yield
                    TT("dve", Cm[d][:, 3, 0:129], Cm[d][:, 3, 0:129], psf[5][:, 256:385], ALU.add,
                       [cmk[3], ("PS", 5, "u3")], [cmk[3]])
                    yield

            P.barrier()
            if not reuse:
                P.sched([normp(border[0], 0)])
                g0 = [prep(border[0], 0)]
                if len(border) > 1:
                    g0.append(normp(border[1], 1))
                P.sched(g0)
            for idx, bi in enumerate(border):
                gens = [chain(bi, idx % 2)]
                if idx + 1 < len(border):
                    gens.append(prep(border[idx + 1], (idx + 1) % 2))
                if idx + 2 < len(border):
                    gens.append(normp(border[idx + 2], idx % 2))
                P.sched(gens)
            P.barrier()

        bg_rot = Rot(2)

        def bgps():
            i = 4 + bg_rot.next()
            return psf[i], ("PS", i, "bg")

        def main_pass(l, res, rkey, tlen, mb, rl, hp0=None, reuse_kvg=False, reuse_g0=False):
            nb = min(512, tlen)
            nblk = tlen // nb
            ncb = nb // 128
            rows = nb // rl
            Gc, Sc, gk = G1[:, l * 3 + mb, :], modc(l, mb, 0), ("G", l, mb)

            def normg(bi, hp, psfn=None):
                t0 = bi * nb
                return norm_g(res[:, :, t0:t0 + nb], (lambda kc, bi=bi: rkey(bi, kc)), nb, Gc, Sc, gk,
                              out_bf=hTs[hp], out_key=lambda kc, hp=hp: ("hT", hp, kc), psfn=psfn)

            P.barrier()
            if hp0 is None:
                hp0 = 0
                for _ in normg(0, 0):
                    pass
            for bi in range(nblk):
                t0 = bi * nb
                hp = (bi + hp0) % 2
                gst = 1 if (reuse_g0 and bi == 0) else 0
                rk = lambda kc, bi=bi: rkey(bi, kc)
                w_ap, wk = load_cg(l, 0, 512)
                for ci in range(ncb):
                    if reuse_kvg:
                        break
                    ps, pk = bigps()
                    proj_tok(w_ap, wk, 0, 512, ci, ps, pk, hp)
                    ACT(ktok[:, ci, :], ps, AF.Copy, [pk], [("ktok", ci)], scale=128.0 ** -0.5)
                for h in range(4):
                    ps, pk = bigps()
                    proj_feat(w_ap, wk, h * 128, nb, ps[:, 0:nb], pk, hp)
                    ACT(kT[:, h, 0:nb], ps[:, 0:nb], AF.Copy, [pk], [("kT", h)], scale=128.0 ** -0.5)
                if not reuse_kvg:
                    w_ap, wk = load_cg(l, 512, 512)
                    for ci in range(ncb):
                        ps, pk = bigps()
                        proj_tok(w_ap, wk, 0, 512, ci, ps, pk, hp)
                        CP("dve", vtok[:, ci, :], ps, [pk], [("vtok", ci)])
                w_ap, wk = load_cg(l, 1024, 528)
                for ci in range(ncb):
                    if reuse_kvg or gst == 1:
                        break
                    bank, bno, o = gsc(ci)
                    proj_tok(w_ap, wk, 0, 16, ci, bank[:, o:o + 16], ("PS", bno, "gps", ci), hp)
                for h in range(4):
                    ps, pk = bigps()
                    proj_feat(w_ap, wk, 16 + h * 128, nb, ps[:, 0:nb], pk, hp)
                    CP("dve", qT[:, h, 0:nb], ps[:, 0:nb], [pk], [("qT", h)])
                w_ap, wk = load_cg(l, 1552, 512)
                for ci in range(ncb):
                    ps, pk = bigps()
                    proj_tok(w_ap, wk, 0, 512, ci, ps, pk, hp)
                    ACT(etmp, ps, AF.Exp, [pk], ["etmp"], scale=-1.0)
                    ACT(etmp, etmp, AF.Ln, ["etmp", "cst"], ["etmp"], bias=one_c)
                    ACT(osig[:, ci, :], etmp, AF.Exp, ["etmp"], [("osig", ci)], scale=-1.0)
                tq0 = max(P.efree.values())

                wout_cg = []

                def bg_gen(bi=bi, hp=hp):
                    w_ap, wk = load_cg(l, 2576, 512)
                    yield
                    for i in range(4):
                        ps, pk = bgps()
                        proj_feat(w_ap, wk, i * 128, nb, ps[:, 0:nb], pk, hp)
                        yield
                        if i < 2:
                            ACT(hbuf[:, i, 0:nb], ps[:, 0:nb], AF.Copy, [pk], [("hbuf", i)])
                        else:
                            ACT(ubuf[:, i - 2, 0:nb], ps[:, 0:nb], AF.Copy, [pk], [("ubuf", i - 2)])
                        yield
                    w_ap, wk = load_cg(l, 2064, 512)
                    yield
                    zsel, zkey = (zp, "zp") if rl == 64 else (zpc, "zpc")
                    for i in range(2):
                        ps, pk = bgps()
                        proj_feat(w_ap, wk, (2 + i) * 128, nb, ps[:, 0:nb], pk, hp)
                        yield
                        zv = zsel[0][:, 0:rows * (rl + 2)].rearrange("p (r c) -> p r c", c=rl + 2)
                        TT("dve", zv[:, :, 1:rl + 1], ps[:, 0:nb].rearrange("p (r c) -> p r c", c=rl),
                           hbuf[:, i, 0:nb].rearrange("p (r c) -> p r c", c=rl), ALU.mult, [pk, ("hbuf", i)], [(zkey, 0)])
                        yield
                        yv = hbuf[:, i, 0:nb].rearrange("p (r c) -> p r c", c=rl)
                        cw = lambda kk, i=i: cvw[:, l * 6 + i * 3 + kk:l * 6 + i * 3 + kk + 1]
                        TS("dve", yv, zv[:, :, 0:rl], cw(0), None, ALU.mult, None, [(zkey, 0), "cvw"], [("hbuf", i)])
                        yield
                        STT(yv, zv[:, :, 1:rl + 1], cw(1), yv, ALU.mult, ALU.add, [(zkey, 0), "cvw", ("hbuf", i)], [("hbuf", i)])
                        yield
                        STT(yv, zv[:, :, 2:rl + 2], cw(2), yv, ALU.mult, ALU.add, [(zkey, 0), "cvw", ("hbuf", i)], [("hbuf", i)])
                        yield
                    for i in range(2):
                        ps, pk = bgps()
                        proj_feat(w_ap, wk, i * 128, nb, ps[:, 0:nb], pk, hp)
                        yield
                        TT("dve", mixT[:, 4 + i, 0:nb], ps[:, 0:nb], hbuf[:, i, 0:nb], ALU.mult, [pk, ("hbuf", i)],
                           [("mix", 4 + i, c) for c in range(ncb)])
                        yield
                    w_ap, wk = load_cg(l, 3088, 256)
                    yield
                    for ci in range(ncb):
                        ps, pk = bgps()
                        proj_tok(w_ap, wk, 0, 256, ci, ps[:, 0:256], pk, hp)
                        yield
                        vb = ci % 2
                        ACT(vn[vb], ps[:, 0:256], AF.Square, [pk], [("vn", vb), ("gss", vb)], accum=gss[vb])
                        yield
                        ACT(gl[vb], gss[vb], AF.Ln, [("gss", vb), "cst"], [("gl", vb)], bias=eps_c, scale=1.0 / 256)
                        yield
                        ACT(gr[vb], gl[vb], AF.Exp, [("gl", vb)], [("gr", vb)], scale=-0.5)
                        yield
                        STT(vn[vb], ps[:, 0:256], gr[vb], gmn, ALU.mult, ALU.mult, [pk, ("gr", vb), "gmn"], [("vn", vb)])
                        yield
                        ps2, pk2 = bgps()
                        for i in range(2):
                            for hf in range(2):
                                g = 2 * i + hf
                                MM(ps2[64 * hf:64 * hf + 64, i * 128:(i + 1) * 128], vn[vb][:, g * 64:(g + 1) * 64],
                                   gmws[:, g * 128:(g + 1) * 128], True, True, [("vn", vb), "gmws"], [pk2])
                        yield
                        for i in range(2):
                            TT("dve", gmtmp[i], ps2[:, i * 128:(i + 1) * 128], gmb[:, i * 128:(i + 1) * 128], ALU.add,
                               [pk2, "gmb"], [("gmtmp", i)])
                            yield
                            TT("dve", mixT[:, 6 + i, ci * 128:(ci + 1) * 128], gmtmp[i], ubuf[:, i, ci * 128:(ci + 1) * 128],
                               ALU.mult, [("gmtmp", i), ("ubuf", i)], [("mix", 6 + i, ci)])
                            yield
                    wout_cg.append(load_cg(l, 0, 512, wd=wout_d))
                    yield
                    wout_cg.append(load_cg(l, 512, 512, wd=wout_d))
                    yield

                bg = [[bg_gen(), None]]
                if not reuse_kvg and gst == 0:
                    P.sched([gate_math_g(ci) for ci in range(ncb)])
                tq1 = max(P.efree.values())

                giF = {}

                def gen_S(ci):
                    j = bi * ncb + ci
                    par = j % 2
                    gi = recur(ci, 0, gst)
                    giF[ci] = gi
                    yield
                    vt3 = vtok[:, ci, :].rearrange("p (h c) -> p h c", c=128)
                    for d, wf_ap, wfkey in ((0, wfc[gi][:, 0:4], ("wfc", gi)), (1, wfb_st[:, j, 0:4], ("wfb", j))):
                        ve = vext[d][par]
                        vk = ("vext", d, par)
                        TT("dve", ve[:, :, 0:128], vt3, wf_ap.unsqueeze(2).to_broadcast([128, 4, 128]), ALU.mult,
                           [("vtok", ci), wfkey], [vk])
                        yield
                        CP("dve", ve[:, :, 128:129], wf_ap.rearrange("p (h o) -> p h o", o=1), [wfkey], [vk])
                        yield
                    cmk = [("Cm", 0, h) for h in range(4)]
                    TT("dve", Cm[0], Cm[0], ee[gi][:, 0:4].unsqueeze(2).to_broadcast([128, 4, 130]), ALU.mult,
                       cmk + [("ee", gi)], cmk)
                    yield
                    cf = cdf[par]
                    ACT(cf, Cm[0], AF.Copy, cmk, [("cdf", par, h) for h in range(4)])
                    yield
                    vef = vext[0][par]
                    for h in range(4):
                        if h < 3:
                            ups = psf[6][:, h * 129:(h + 1) * 129]
                            uk = ("PS", 6, "u")
                        else:
                            ups = psf[1][:, 256:385]
                            uk = ("PS", 1, "u3")
                        MM(ups, ktok[:, ci, h * 128:(h + 1) * 128], vef[:, h, 0:129], True, True,
                           [("ktok", ci), ("vext", 0, par)], [uk])
                        yield
                    TT("dve", Cm[0][:, 0:3, 0:129], Cm[0][:, 0:3, 0:129],
                       psf[6][:, 0:387].rearrange("p (h c) -> p h c", c=129), ALU.add,
                       cmk[0:3] + [("PS", 6, "u")], cmk[0:3])
                    yield
                    TT("dve", Cm[0][:, 3, 0:129], Cm[0][:, 3, 0:129], psf[1][:, 256:385], ALU.add,
                       [cmk[3], ("PS", 1, "u3")], [cmk[3]])
                    yield

                def nd_ap(h):
                    b = psf[2 + h // 2]
                    o = (h % 2) * 256
                    return b[:, o:o + 128], b[:, o + 128:o + 256], ("PS", 2 + h // 2, "f", h % 2), ("PS", 2 + h // 2, "b", h % 2)

                def gen_H(ci, h):
                    j = bi * ncb + ci
                    par = j % 2
                    cb = ci * 128
                    vef, vkf = vext[0][par], ("vext", 0, par)
                    veb, vkb = vext[1][par], ("vext", 1, par)
                    cf = cdf[par]
                    sps = psf[h // 2][:, (h % 2) * 128:(h % 2 + 1) * 128]
                    sk = ("PS", h // 2, "st", h % 2)
                    MM(sps, kT[:, h, cb:cb + 128], qT[:, h, cb:cb + 128], True, True, [("kT", h), ("qT", h)], [sk])
                    yield
                    TT("dve", Sfb[0][:, h, :], sps, maskU, ALU.mult, [sk, "maskU"], [("Sf", h)])
                    yield
                    TT("dve", Sfb[1][:, h, :], sps, maskL, ALU.mult, [sk, "maskL"], [("Sb", h)])
                    yield
                    ndf, ndb, nkf, nkb = nd_ap(h)
                    dnf = psf[6][:, 400 + h:401 + h]
                    dnb = psf[6][:, 404 + h:405 + h]
                    dk = ("PS", 6, "den")
                    qc = qT[:, h, cb:cb + 128]
                    MM(ndf, Sfb[0][:, h, :], vef[:, h, 0:128], True, False, [("Sf", h), vkf], [nkf])
                    MM(ndf, qc, cf[:, h, 0:128], False, True, [("qT", h), ("cdf", par, h)], [nkf])
                    yield
                    MM(dnf, Sfb[0][:, h, :], vef[:, h, 128:129], True, False, [("Sf", h), vkf], [dk])
                    MM(dnf, qc, cf[:, h, 128:129], False, True, [("qT", h), ("cdf", par, h)], [dk])
                    yield
                    MM(ndb, Sfb[1][:, h, :], veb[:, h, 0:128], True, False, [("Sb", h), vkb], [nkb])
                    MM(ndb, qc, cdb[:, j * 4 + h, 0:128], False, True, [("qT", h), ("cdb", j, h)], [nkb])
                    yield
                    MM(dnb, Sfb[1][:, h, :], veb[:, h, 128:129], True, False, [("Sb", h), vkb], [dk])
                    MM(dnb, qc, cdb[:, j * 4 + h, 128:129], False, True, [("qT", h), ("cdb", j, h)], [dk])
                    yield

                def gen_T1(ci):
                    j = bi * ncb + ci
                    par = j % 2
                    gi = giF[ci]
                    dk = ("PS", 6, "den")
                    ACT(dden8[par], psf[6][:, 400:408], AF.Abs, [dk], [("dden", par)])
                    yield
                    TT("dve", dden8[par][:, 0:4], dden8[par][:, 0:4], wfc[gi][:, 4:8], ALU.max,
                       [("dden", par), ("wfc", gi)], [("dden", par)])
                    TT("dve", dden8[par][:, 4:8], dden8[par][:, 4:8], wfb_st[:, j, 4:8], ALU.max,
                       [("dden", par), ("wfb", j)], [("dden", par)])
                    yield
                    RCP(rr8[par], dden8[par], [("dden", par)], [("rr", par)])
                    yield
                    for h in range(4):
                        ndf, ndb, nkf, nkb = nd_ap(h)
                        hsl = hs2[par][:, h * 128:(h + 1) * 128]
                        hk = ("hs", par, h)
                        ACT(hsl, ndf, AF.Copy, [nkf, ("rr", par)], [hk], scale=rr8[par][:, h:h + 1])
                        yield
                        STT(hsl, ndb, rr8[par][:, 4 + h:5 + h], hsl, ALU.mult, ALU.add, [nkb, ("rr", par), hk], [hk])
                        yield

                def gen_T2(ci):
                    j = bi * ncb + ci
                    par = j % 2
                    cb = ci * 128
                    for h in range(4):
                        hsl = hs2[par][:, h * 128:(h + 1) * 128]
                        ACT(otok2[par][:, h * 128:(h + 1) * 128], hsl, AF.Square, [("hs", par, h)],
                            [("otok", par, h), ("ssq", par, h)], accum=ssq2[par][:, h:h + 1])
                        yield
                    ACT(lnv2[par], ssq2[par], AF.Ln, [("ssq", par, h) for h in range(4)] + ["cst"], [("lnv", par)],
                        bias=eps_c, scale=1.0 / 128)
                    yield
                    ACT(rsv2[par], lnv2[par], AF.Exp, [("lnv", par)], [("rsv", par)], scale=-0.5)
                    yield
                    TT("dve", go, osig[:, ci, :], mln, ALU.mult, [("osig", ci), "mln"], ["etmp"])
                    yield
                    for h in range(4):
                        STT(otok2[par][:, h * 128:(h + 1) * 128], hs2[par][:, h * 128:(h + 1) * 128], rsv2[par][:, h:h + 1],
                            go[:, h * 128:(h + 1) * 128], ALU.mult, ALU.mult, [("hs", par, h), ("rsv", par), "etmp"],
                            [("otok", par, h)])
                        yield
                    for h in range(4):
                        TR(psb[:, par * 512 + h * 128:par * 512 + (h + 1) * 128], otok2[par][:, h * 128:(h + 1) * 128], identb,
                           [("otok", par, h), "identb"], [("PS", 7, par)])
                        yield
                    ACT(mixT[:, 0:4, cb:cb + 128], psb[:, par * 512:(par + 1) * 512].rearrange("p (h t) -> p h t", t=128),
                        AF.Copy, [("PS", 7, par)], [("mix", h, ci) for h in range(4)])
                    yield

                P.sched([gen_S(0)], bg)
                for ci in range(ncb):
                    gens = [gen_H(ci, h) for h in range(4)]
                    if ci + 1 < ncb:
                        gens.append(gen_S(ci + 1))
                    if ci >= 1:
                        gens.append(gen_T2(ci - 1))
                    P.sched(gens, bg)
                    P.sched([gen_T1(ci)], bg)
                P.sched([gen_T2(ncb - 1)], bg)
                tq2 = max(P.efree.values())
                P.sched([], bg, drain=True)
                tq3 = max(P.efree.values())
                def wout_g():
                    for dt in range(KC):
                        w_ap, wk = wout_cg[dt // 4]
                        ps, pk = bigps()
                        for jj in range(KC):
                            MM(ps[:, 0:nb], w_ap[:, jj, (dt % 4) * 128:(dt % 4 + 1) * 128], mixT[:, jj, 0:nb], jj == 0, jj == KC - 1,
                               [wk] + [("mix", jj, c) for c in range(ncb)], [pk])
                        yield
                        STT(res[:, dt, t0:t0 + nb], ps[:, 0:nb], modc(l, mb, 2)[:, dt:dt + 1], res[:, dt, t0:t0 + nb],
                            ALU.mult, ALU.add, [pk, ("mod", l, mb), rk(dt)], [rk(dt)])
                        yield

                nbg = []
                if bi + 1 < nblk:
                    nbg = [[normg(bi + 1, 1 - hp, psfn=lambda: (psf[6], ("PS", 6, "ss"))), None]]
                P.sched([wout_g()], nbg)
                P.sched([], nbg, drain=True)
                tq4 = max(P.efree.values())
                if l == 0 and nblk == 4 and mb == 0:
                    print("main blk", bi, "pre_end %.0f gates %.0f chunks %.0f drain %.0f wout %.0f" % (tq0, tq1 - tq0, tq2 - tq1, tq3 - tq2, tq4 - tq3))
            P.barrier()

        def ffn(l, sbs):
            m = A.off
            A.off = mark_ovl
            tot = max(sum(p[2] for p in parts) for parts in sbs)
            hT2 = A.bf16(KC, tot)
            gT = A.bf16(FC, tot)
            w13 = [[A.bf16(KC, 256), A.bf16(KC, 256)] for _ in range(2)]
            w2b = [A.bf16(FC, 128), A.bf16(FC, 128)]
            stmp = [A.f32(512), A.f32(512)]
            wr = Rot(2)
            sr = Rot(2)

            def offsets(parts):
                offs, o = [], 0
                for p in parts:
                    offs.append(o)
                    o += p[2]
                return offs

            def gk(nm, a, o, n):
                return [(nm, a, g) for g in range(o // 256, (o + n) // 256)]

            def norms_g(parts):
                offs = offsets(parts)
                for (res, rk, n, mb), o in zip(parts, offs):
                    yield from norm_g(res, rk, n, G2[:, l * 3 + mb, :], modc(l, mb, 3), ("G", l, mb),
                                      out_bf=hT2[:, :, o:o + n], out_key=lambda kc, o=o, n=n: gk("hT2", kc, o, n))

            def f1_g(parts):
                offs = offsets(parts)
                for fg in range(FC // 2):
                    bi = wr.next()
                    for wi, wd in enumerate((w1_d, w3_d)):
                        src = wd[l].rearrange("(kc p) c -> p kc c", p=128)[:, :, fg * 256:(fg + 1) * 256]
                        DMA("pool", w13[bi][wi], src, "f%d%d" % (bi, wi), (), [("w13", bi, wi)])
                    yield
                    for f2 in range(2):
                        f = fg * 2 + f2
                        for pi, (res, rk, n, mb) in enumerate(parts):
                            o = offs[pi]
                            p1, k1 = bigps()
                            for kc in range(KC):
                                MM(p1[:, 0:n], w13[bi][0][:, kc, f2 * 128:(f2 + 1) * 128], hT2[:, kc, o:o + n], kc == 0,
                                   kc == KC - 1, [("w13", bi, 0)] + gk("hT2", kc, o, n), [k1])
                            yield
                            p3, k3 = bigps()
                            for kc in range(KC):
                                MM(p3[:, 0:n], w13[bi][1][:, kc, f2 * 128:(f2 + 1) * 128], hT2[:, kc, o:o + n], kc == 0,
                                   kc == KC - 1, [("w13", bi, 1)] + gk("hT2", kc, o, n), [k3])
                            yield
                            si = sr.next()
                            ACT(stmp[si][:, 0:n], p1[:, 0:n], AF.Silu, [k1], [("stmp", si)])
                            yield
                            TT("dve", gT[:, f, o:o + n], stmp[si][:, 0:n], p3[:, 0:n], ALU.mult, [("stmp", si), k3],
                               gk("gT", f, o, n))
                            yield

            def f2_g(parts):
                offs = offsets(parts)
                for dt in range(KC):
                    bi = wr.next()
                    src = w2_d[l].rearrange("(fc p) c -> p fc c", p=128)[:, :, dt * 128:(dt + 1) * 128]
                    DMA("pool", w2b[bi], src, "g%d" % bi, (), [("w2b", bi)])
                    yield
                    for pi, (res, rk, n, mb) in enumerate(parts):
                        o = offs[pi]
                        ps, pk = bigps()
                        for f in range(FC):
                            MM(ps[:, 0:n], w2b[bi][:, f, :], gT[:, f, o:o + n], f == 0, f == FC - 1,
                               [("w2b", bi)] + gk("gT", f, o, n), [pk])
                        yield
                        STT(res[:, dt, :], ps[:, 0:n], modc(l, mb, 5)[:, dt:dt + 1], res[:, dt, :], ALU.mult, ALU.add,
                            [pk, ("mod", l, mb), rk(dt)], [rk(dt)])
                        yield

            P.barrier()
            P.sched([norms_g(sbs[0])])
            P.sched([f1_g(sbs[0])])
            for k in range(len(sbs)):
                gens = [f2_g(sbs[k])]
                if k + 1 < len(sbs):
                    gens.append(norms_g(sbs[k + 1]))
                P.sched(gens)
                if k + 1 < len(sbs):
                    P.sched([f1_g(sbs[k + 1])])
            P.barrier()
            A.off = m

        xkey = lambda bi, kc: ("x", bi, kc)
        ckey = lambda bi, kc: ("c", bi, kc)
        for s in range(n_seq):
            for kc in range(KC):
                DMA("sp", xT[:, kc, :], xT_d[s, kc * 128:(kc + 1) * 128, :], "ld%d" % ld_rot.next(), (), [xkey(bi, kc) for bi in range(4)])
            DMA("sp", ctxT, ctxT_d[s].rearrange("(kc p) t -> p kc t", p=128), "ld%d" % ld_rot.next(), (), [ckey(0, kc) for kc in range(KC)])
            for l in range(n_layers):
                last = l == L - 1
                DMA("sp", bgb, bg_d[l], "ld%d" % ld_rot.next(), (), ["bgb"])
                DMA("sp", mln, mln_d[l], "ld%d" % ld_rot.next(), (), ["mln"])
                DMA("sp", gmn, gmn_d[l], "ld%d" % ld_rot.next(), (), ["gmn"])
                DMA("sp", gmb, gmb_d[l], "ld%d" % ld_rot.next(), (), ["gmb"])
                DMA("pool", gmws, gmws_d[l], "gm", (), ["gmws"])
                for d in range(2):
                    MSET("dve", Cm[d], 0.0, [("Cm", d, h) for h in range(4)])
                    MSET("dve", mst[d], 0.0, [("mst", d)])
                if lvl >= 2:
                    state_pass(l, ctxT, ckey, TC, 2, 1)
                mark("ctx_state")
                if not last:
                    if lvl >= 3:
                        main_pass(l, ctxT, ckey, TC, 2, TC, hp0=(0 if lvl >= 2 else None), reuse_kvg=(lvl >= 2))
                else:
                    state_pass(l, ctxT, ckey, TC, 2, 0, reuse=(lvl >= 2))
                mark("ctx_main")
                if lvl >= 4:
                    state_pass(l, xT, xkey, T, s, 1)
                mark("x_state")
                if lvl >= 5:
                    main_pass(l, xT, xkey, T, s, 64, hp0=(1 if lvl >= 4 else None), reuse_g0=(lvl >= 4))
                mark("x_main")
                P.barrier()
                if lvl < 6:
                    continue
                xparts = [(xT[:, :, bi * 512:(bi + 1) * 512], (lambda kc, bi=bi: xkey(bi, kc)), 512, s) for bi in range(4)]
                cpart = (ctxT, (lambda kc: ckey(0, kc)), TC, 2)
                if not last:
                    ffn(l, [[cpart] + xparts[0:2], xparts[2:4]])
                else:
                    ffn(l, [xparts[0:2], xparts[2:4]])
                mark("ffn")
            if do_final:
                for bi in range(4):
                    norm_block(xT[:, :, bi * 512:(bi + 1) * 512], (lambda kc, bi=bi: xkey(bi, kc)), 512, nf, None, "nf",
                               inplace=True)
            if dump_ctx:
                for kc in range(KC):
                    DMA("sp", out_d[s, kc * 128:(kc + 1) * 128, 0:TC], ctxT[:, kc, :], "o%d" % (kc % 4), [ckey(0, kc)], ())
            for kc in range(KC):
                if dump_ctx:
                    break
                DMA("sp", out_d[s, kc * 128:(kc + 1) * 128, :], xT[:, kc, :], "o%d" % (kc % 4), [xkey(bi, kc) for bi in range(4)], ())
            P.barrier()

        if dump_arena:
            P.barrier()
            for i in range(8):
                DMA("sp", dbg_d[:, i * 6650:(i + 1) * 6650], arena_t[:, i * 6650:(i + 1) * 6650], "o%d" % (i % 4), (), ())
            P.barrier()
            byid = {e[0]: e for e in A.log}
            names = {}

            def visit(nm, v):
                if isinstance(v, list):
                    for i, x in enumerate(v):
                        visit("%s.%d" % (nm, i), x)
                elif id(v) in byid:
                    e = byid[id(v)]
                    names[nm] = (e[1], e[2], tuple(int(x) for x in e[3]))
            for nm, v in list(locals().items()):
                visit(nm, v)
            nc._arena_names = names
        P.prepare()
        for k in P.dma_n:
            dma_sems[k] = es.enter_context(nc.semaphore("d_" + k))
        with nc.allow_low_precision(reason="bf16 matmul operands, fp32 accumulation"):
            @block.tensor
            def _(e):
                P.emit_engine("pe", e, sems, dma_sems)

            @block.scalar
            def _(e):
                P.emit_engine("act", e, sems, dma_sems)

            @block.vector
            def _(e):
                P.emit_engine("dve", e, sems, dma_sems)

            @block.gpsimd
            def _(e):
                P.emit_engine("pool", e, sems, dma_sems)

            @block.sync
            def _(e):
                P.emit_engine("sp", e, sems, dma_sems)
        prev = 0.0
        acc_t = {}
        for nm, t in marks:
            acc_t[nm] = acc_t.get(nm, 0.0) + (t - prev)
            prev = t
        print("phase est us:", {k: round(v) for k, v in acc_t.items()})
        print("ops:", {e: len(P.q[e]) for e in ENG}, "arena words:", A.off, "est_us:", {e: round(P.efree[e]) for e in ENG})
    return nc


def _host_inputs(inp):
    f = np.float32
    x = np.asarray(inp["x"], f)
    c = np.asarray(inp["c"], f)
    ctx = np.asarray(inp["ctx"], f)
    c_ctx = np.asarray(inp["c_ctx"], f)

    def cols(v):
        return np.ascontiguousarray(v.reshape(-1, 128).T)

    shared = {
        "w_ada": np.ascontiguousarray(inp["w_ada"], f),
        "bada": cols(np.asarray(inp["b_ada"], f)),
        "n1": cols(np.asarray(inp["norm1"], f)),
        "n2": cols(np.asarray(inp["norm2"], f)),
        "nf": cols(np.asarray(inp["norm_f"], f)),
        "w_in": np.ascontiguousarray(inp["w_in"], f),
        "bg": np.ascontiguousarray(np.broadcast_to(np.asarray(inp["b_gates"], f)[:, None, :], (L, 128, 16))),
        "mln": np.ascontiguousarray(np.broadcast_to(np.asarray(inp["ml_norm"], f).reshape(L, 1, 512), (L, 128, 512))),
        "cvw": np.ascontiguousarray(np.asarray(inp["conv_w"], f).reshape(L, 3, 2, 128).transpose(3, 0, 2, 1).reshape(128, L * 6)),
        "gmn": np.ascontiguousarray(np.broadcast_to(np.asarray(inp["gm_norm"], f)[:, None, :], (L, 128, 256))),
        "gmws": np.ascontiguousarray(np.asarray(inp["gm_ws"], f).transpose(0, 3, 1, 2).reshape(L, 128, 512)),
        "gmb": np.ascontiguousarray(
            np.repeat(np.asarray(inp["gm_bs"], f).reshape(L, 2, 2, 1, 128), 64, axis=3).reshape(L, 2, 128, 128)
            .transpose(0, 2, 1, 3).reshape(L, 128, 256)),
        "w_out": np.ascontiguousarray(inp["w_out"], f),
        "w1": np.ascontiguousarray(inp["w1"], f),
        "w3": np.ascontiguousarray(inp["w3"], f),
        "w2": np.ascontiguousarray(inp["w2"], f),
    }
    maps = []
    for i in range(NCORES):
        b0 = 2 * i
        cc = np.stack([c[b0], c[b0 + 1], c_ctx], axis=1)
        cT = np.ascontiguousarray(cc.reshape(8, 128, 3).transpose(1, 0, 2).reshape(128, 24))
        m = dict(shared)
        m["xT"] = np.ascontiguousarray(x[b0:b0 + 2].transpose(0, 2, 1))
        m["ctxT"] = np.ascontiguousarray(ctx[b0:b0 + 2].transpose(0, 2, 1))
        m["cT"] = cT
        maps.append(m)
    return maps


_NC_CACHE = {}


def kernel(**inputs):
    maps = _host_inputs(inputs)
    if "nc" not in _NC_CACHE:
        _NC_CACHE["nc"] = build_program()
    nc = _NC_CACHE["nc"]
    res = run_bass_kernel_spmd(nc, maps, core_ids=list(range(NCORES)))
    out = np.empty((2 * NCORES, T, D), np.float32)
    for i in range(NCORES):
        o = np.asarray(res.results[i]["outT"])
        out[2 * i:2 * i + 2] = o.transpose(0, 2, 1)
    return out
```
